# Optimizing a Trainium2 kernel written in Bass

```python
import jax, jax.numpy as jnp
from jax import lax
import numpy as np

D_MODEL = 1024
BATCH = 16
SEQ = 2048
DEPTH = 2
DEC_BATCH = 8
DEC_SEQ = 8192
PAST_LEN = 128

SG_HEADS = 4
SG_WIDTH = D_MODEL // 2
SG_HEAD_DIM = SG_WIDTH // SG_HEADS
SG_CHUNK = 128
GLA_HEADS = 4
GLA_K_WIDTH = D_MODEL // 4
GLA_V_WIDTH = D_MODEL // 2
GLA_DK = GLA_K_WIDTH // GLA_HEADS
GLA_DV = GLA_V_WIDTH // GLA_HEADS
GLA_CHUNK = 64
GLA_GATE_RANK = 16
GLA_GATE_NORMALIZER = 16.0
SPLIT_SIZES = (SG_WIDTH, SG_WIDTH, GLA_K_WIDTH, GLA_K_WIDTH, GLA_V_WIDTH, GLA_V_WIDTH, GLA_GATE_RANK, GLA_GATE_RANK)
SPLIT_IDX = tuple(int(i) for i in np.cumsum(SPLIT_SIZES)[:-1])
IN_COLS = int(sum(SPLIT_SIZES))
D_MIX = SG_WIDTH + GLA_V_WIDTH
D_FF = ((8 * D_MODEL // 3 + 127) // 128) * 128
CONV_WIDTH = 3
N_MOD = 6
EPS = 1e-6

kernel_name = "hymba_style_sg_gla_bidir_encoder"


def rmsnorm(x, g):
    xf = x.astype(jnp.float32)
    y = xf * lax.rsqrt(jnp.mean(xf * xf, axis=-1, keepdims=True) + EPS)
    return (y * g.astype(jnp.float32)).astype(x.dtype)


def layernorm(x, g):
    xf = x.astype(jnp.float32)
    xc = xf - jnp.mean(xf, axis=-1, keepdims=True)
    y = xc * lax.rsqrt(jnp.mean(xc * xc, axis=-1, keepdims=True) + EPS)
    return (y * g.astype(jnp.float32)).astype(x.dtype)


def spatial_gating(u, v, w_s, b_s, g_vn):
    B, T, _ = u.shape
    n = T // SG_CHUNK
    shp = (B, n, SG_CHUNK, SG_HEADS, SG_HEAD_DIM)
    vn = layernorm(v.reshape(shp), g_vn.reshape(SG_HEADS, SG_HEAD_DIM))
    mixed = jnp.einsum('hij,bnjhc->bnihc', w_s, vn) + b_s.T[:, :, None]
    return (u.reshape(shp) * mixed).reshape(B, T, SG_WIDTH)


def gla_chunked(q, k, v, g):
    B, T, H, _ = q.shape
    n = T // GLA_CHUNK
    rs = lambda a: a.reshape(B, n, GLA_CHUNK, H, a.shape[-1])
    q, k, v, g = rs(q), rs(k), rs(v), rs(g)
    b = jnp.cumsum(g, axis=2)
    b_last = b[:, :, -1:]
    q_dec = q * jnp.exp(b)
    k_dec = k * jnp.exp(-b)
    mask = jnp.tril(jnp.ones((GLA_CHUNK, GLA_CHUNK), dtype=bool))
    scores = jnp.where(mask, jnp.einsum('bnihd,bnjhd->bnhij', q_dec, k_dec), 0.0)
    o_intra = jnp.einsum('bnhij,bnjhv->bnihv', scores, v)
    d_state = jnp.einsum('bnjhd,bnjhv->nbhdv', k * jnp.exp(b_last - b), v)
    decay = jnp.exp(b_last[:, :, 0]).transpose(1, 0, 2, 3)
    q_seq = q_dec.transpose(1, 0, 2, 3, 4)

    def step(S, inp):
        qn, dn, dSn = inp
        o = jnp.einsum('bihd,bhdv->bihv', qn, S)
        return dn[..., None] * S + dSn, o

    S0 = jnp.zeros((B, H, GLA_DK, GLA_DV), jnp.float32)
    _, o_inter = lax.scan(step, S0, (q_seq, decay, d_state))
    o = o_intra + o_inter.transpose(1, 0, 2, 3, 4)
    return o.reshape(B, T, H, GLA_DV)


def mixer(h, w_in, w_s, b_s, g_vn, g_out_a, w_gf, b_gf, w_gb, b_gb, g_out_b, w_out):
    B, T, _ = h.shape
    f32 = jnp.float32
    proj = h @ w_in
    u_a, v_a, q, k, v_b, r, lf, lb = jnp.split(proj, SPLIT_IDX, axis=-1)
    a = spatial_gating(u_a, v_a, w_s, b_s, g_vn)
    a = rmsnorm(a.reshape(B, T, SG_HEADS, SG_HEAD_DIM), g_out_a.reshape(SG_HEADS, SG_HEAD_DIM)).reshape(B, T, SG_WIDTH)
    qh = q.reshape(B, T, GLA_HEADS, GLA_DK).astype(f32) * (GLA_DK ** -0.5)
    kh = k.reshape(B, T, GLA_HEADS, GLA_DK).astype(f32)
    vh = v_b.reshape(B, T, GLA_HEADS, GLA_DV).astype(f32)
    gf = (jax.nn.log_sigmoid((lf @ w_gf + b_gf).astype(f32)) / GLA_GATE_NORMALIZER).reshape(B, T, GLA_HEADS, GLA_DK)
    gb = (jax.nn.log_sigmoid((lb @ w_gb + b_gb).astype(f32)) / GLA_GATE_NORMALIZER).reshape(B, T, GLA_HEADS, GLA_DK)
    flip = lambda t: jnp.flip(t, axis=1)
    o_fwd = gla_chunked(qh, kh, vh, gf)
    o_bwd = flip(gla_chunked(flip(qh), flip(kh), flip(vh), flip(gb)))
    o = rmsnorm(o_fwd + o_bwd, g_out_b.reshape(GLA_HEADS, GLA_DV)).reshape(B, T, GLA_V_WIDTH)
    o = (o * jax.nn.silu(r.astype(f32))).astype(h.dtype)
    return jnp.concatenate([a.astype(h.dtype), o], axis=-1) @ w_out


def conv_ffn(h, w_up, w_conv, b_conv, w_down):
    up = h @ w_up
    up = lax.conv_general_dilated(
        up, w_conv[:, None, :].astype(up.dtype), window_strides=(1,),
        padding=((CONV_WIDTH // 2, CONV_WIDTH // 2),),
        dimension_numbers=('NWC', 'WIO', 'NWC'),
        feature_group_count=2 * D_FF) + b_conv
    gate, val = jnp.split(up, 2, axis=-1)
    return (jax.nn.silu(gate) * val) @ w_down


def encoder_trunk(x, c, w_ada, b_ada, g_pre_mix, g_post_mix, g_pre_ffn, g_post_ffn,
                  w_in, w_s, b_s, g_vn, g_out_a, w_gf, b_gf, w_gb, b_gb, g_out_b, w_out,
                  w_up, w_conv, b_conv, w_down):
    for l in range(DEPTH):
        mod = jax.nn.silu(c) @ w_ada[l] + b_ada[l]
        shift1, scale1, gate1, shift2, scale2, gate2 = [m[:, None, :] for m in jnp.split(mod, N_MOD, axis=-1)]
        h = rmsnorm(x, g_pre_mix[l]) * (1 + scale1) + shift1
        y = mixer(h, w_in[l], w_s[l], b_s[l], g_vn[l], g_out_a[l], w_gf[l], b_gf[l],
                  w_gb[l], b_gb[l], g_out_b[l], w_out[l])
        x = x + gate1 * rmsnorm(y, g_post_mix[l])
        h = rmsnorm(x, g_pre_ffn[l]) * (1 + scale2) + shift2
        y = conv_ffn(h, w_up[l], w_conv[l], b_conv[l], w_down[l])
        x = x + gate2 * rmsnorm(y, g_post_ffn[l])
    return x


def setup_inputs(seed: int = 0) -> dict:
    key = jax.random.key(seed)
    ks = jax.random.split(key, 28)
    L = DEPTH
    nrm = lambda k, shape, s: jax.random.normal(k, shape, jnp.float32) * s
    gain = lambda k, shape: 1.0 + 0.02 * jax.random.normal(k, shape, jnp.float32)
    return {
        "x_prompt": nrm(ks[0], (BATCH, SEQ, D_MODEL), 1.0),
        "x_sample": nrm(ks[1], (DEC_BATCH, DEC_SEQ, D_MODEL), 1.0),
        "c_prompt": nrm(ks[2], (BATCH, D_MODEL), 1.0),
        "c_sample": nrm(ks[3], (DEC_BATCH, D_MODEL), 1.0),
        "w_ada": nrm(ks[4], (L, D_MODEL, N_MOD * D_MODEL), 0.5 * D_MODEL ** -0.5),
        "b_ada": nrm(ks[5], (L, N_MOD * D_MODEL), 0.02),
        "g_pre_mix": gain(ks[6], (L, D_MODEL)),
        "g_post_mix": gain(ks[7], (L, D_MODEL)),
        "g_pre_ffn": gain(ks[8], (L, D_MODEL)),
        "g_post_ffn": gain(ks[9], (L, D_MODEL)),
        "w_in": nrm(ks[10], (L, D_MODEL, IN_COLS), D_MODEL ** -0.5),
        "w_s": nrm(ks[11], (L, SG_HEADS, SG_CHUNK, SG_CHUNK), SG_CHUNK ** -0.5),
        "b_s": gain(ks[12], (L, SG_HEADS, SG_CHUNK)),
        "g_vn": gain(ks[13], (L, SG_WIDTH)),
        "g_out_a": gain(ks[14], (L, SG_WIDTH)),
        "w_gf": nrm(ks[15], (L, GLA_GATE_RANK, GLA_K_WIDTH), GLA_GATE_RANK ** -0.5),
        "b_gf": nrm(ks[16], (L, GLA_K_WIDTH), 0.1),
        "w_gb": nrm(ks[17], (L, GLA_GATE_RANK, GLA_K_WIDTH), GLA_GATE_RANK ** -0.5),
        "b_gb": nrm(ks[18], (L, GLA_K_WIDTH), 0.1),
        "g_out_b": gain(ks[19], (L, GLA_V_WIDTH)),
        "w_out": nrm(ks[20], (L, D_MIX, D_MODEL), D_MIX ** -0.5),
        "w_up": nrm(ks[21], (L, D_MODEL, 2 * D_FF), D_MODEL ** -0.5),
        "w_conv": nrm(ks[22], (L, CONV_WIDTH, 2 * D_FF), CONV_WIDTH ** -0.5),
        "b_conv": nrm(ks[23], (L, 2 * D_FF), 0.02),
        "w_down": nrm(ks[24], (L, D_FF, D_MODEL), D_FF ** -0.5),
    }


def reference(x_prompt, x_sample, c_prompt, c_sample, w_ada, b_ada, g_pre_mix, g_post_mix,
              g_pre_ffn, g_post_ffn, w_in, w_s, b_s, g_vn, g_out_a, w_gf, b_gf, w_gb, b_gb,
              g_out_b, w_out, w_up, w_conv, b_conv, w_down):
    params = (w_ada, b_ada, g_pre_mix, g_post_mix, g_pre_ffn, g_post_ffn, w_in, w_s, b_s, g_vn,
              g_out_a, w_gf, b_gf, w_gb, b_gb, g_out_b, w_out, w_up, w_conv, b_conv, w_down)
    y_prompt = encoder_trunk(x_prompt, c_prompt, *params)
    y_sample = encoder_trunk(x_sample, c_sample, *params)
    return (y_prompt, y_sample)
```

```python
import os
import numpy as np
from contextlib import ExitStack
import concourse.bass as bass
import concourse.mybir as mybir
from concourse.bass_utils import run_bass_kernel_spmd

F32 = mybir.dt.float32
BF16 = mybir.dt.bfloat16
AF = mybir.ActivationFunctionType
ALU = mybir.AluOpType
AX = mybir.AxisListType

ENGS = ("pe", "act", "dve", "pool", "sp")
D = 1024
DFF = 2816
NT = 512
EPS = 1e-6


class _Buf:
    __slots__ = ("name", "wev", "rev", "dsem", "dcount")

    def __init__(self, name):
        self.name = name
        self.wev = {}
        self.rev = {}
        self.dsem = None
        self.dcount = 0


_BUFS = {}


def Buf(name):
    b = _BUFS.get(name)
    if b is None:
        b = _BUFS[name] = _Buf(name)
    return b


class Prog:
    def __init__(self, nc):
        self.nc = nc
        self.q = {e: [] for e in ENGS}
        self.dma_sems = {}
        self.dma_last = {}
        self.pending = {e: {} for e in ENGS}

    def _deps(self, eng, reads, writes, is_dma):
        waits = dict(self.pending[eng])
        self.pending[eng] = {}

        def need(key, idx):
            if waits.get(key, -1) < idx:
                waits[key] = idx

        for b in reads:
            for key, idx in b.wev.items():
                if key == eng and eng == "pe" and not is_dma:
                    continue
                need(key, idx)
        for b in writes:
            for key, idx in b.wev.items():
                if key == eng and eng == "pe" and not is_dma:
                    continue
                need(key, idx)
            for key, idx in b.rev.items():
                if key == eng and eng == "pe" and not is_dma:
                    continue
                need(key, idx)
        return waits

    def op(self, eng, fn, reads=(), writes=()):
        waits = self._deps(eng, reads, writes, False)
        idx = len(self.q[eng])
        self.q[eng].append(dict(fn=fn, waits=waits, dma=None, ms=False))
        for b in reads:
            b.rev[eng] = idx
        for b in writes:
            b.wev[eng] = idx
        return idx

    def dma(self, eng, fn, reads=(), writes=(), sem_buf=None):
        waits = self._deps(eng, reads, writes, True)
        kind = "sw" if eng == "pool" else "hw"
        if sem_buf.dsem is None:
            sem_buf.dsem = {}
            sem_buf.dcount = {}
        if kind not in sem_buf.dsem:
            sem_buf.dsem[kind] = ("dma", len(self.dma_sems))
            self.dma_sems[sem_buf.dsem[kind]] = None
            sem_buf.dcount[kind] = 0
        sem_buf.dcount[kind] += 16
        ev = (sem_buf.dsem[kind], sem_buf.dcount[kind])
        self.dma_last[ev[0]] = ev[1]
        self.q[eng].append(dict(fn=fn, waits=waits, dma=ev[0], ms=False))
        for b in reads:
            b.rev[ev[0]] = ev[1]
        for b in writes:
            b.wev[ev[0]] = ev[1]
        return ev

    def barrier(self, toks):
        for e in ("act", "dve", "pool"):
            fn, b = toks[e]
            self.op(e, fn, writes=[b])
        w = {}
        for e in ("act", "dve", "pool"):
            w[e] = len(self.q[e]) - 1
        for k, v in self.dma_last.items():
            w[k] = v
        for e in ENGS:
            p = self.pending[e]
            for k, v in w.items():
                if p.get(k, -1) < v:
                    p[k] = v

    def emit(self, final_bufs=()):
        nc = self.nc
        for e in ENGS:
            for o in self.q[e]:
                for key, idx in o["waits"].items():
                    if not isinstance(key, tuple):
                        self.q[key][idx]["ms"] = True
        ticks = {}
        for e in ENGS:
            t = 0
            for i, o in enumerate(self.q[e]):
                if o["ms"] and o["dma"] is None:
                    t += 1
                    ticks[(e, i)] = t
        with ExitStack() as st:
            esem = {e: st.enter_context(nc.semaphore("s_" + e)) for e in ENGS if e != "sp"}
            for k in list(self.dma_sems):
                self.dma_sems[k] = st.enter_context(nc.semaphore("d%d" % k[1]))
            block = st.enter_context(nc.Block())

            def run(e, eo):
                waited = {}

                def dowait(key, idx):
                    if isinstance(key, tuple):
                        sem, val = self.dma_sems[key], idx
                    else:
                        sem, val = esem[key], ticks[(key, idx)]
                    if waited.get(key, -1) >= val:
                        return
                    waited[key] = val
                    eo.wait_ge(sem, val)

                for o in self.q[e]:
                    for key, idx in o["waits"].items():
                        dowait(key, idx)
                    ins = o["fn"](eo)
                    if o["dma"] is not None:
                        ins.then_inc(self.dma_sems[o["dma"]], 16)
                    elif o["ms"]:
                        ins.then_inc(esem[e], 1)
                if e == "sp":
                    for b in final_bufs:
                        for key, idx in b.wev.items():
                            dowait(key, idx)

            @block.tensor
            def _(eo):
                run("pe", eo)

            @block.scalar
            def _(eo):
                run("act", eo)

            @block.vector
            def _(eo):
                run("dve", eo)

            @block.gpsimd
            def _(eo):
                run("pool", eo)

            @block.sync
            def _(eo):
                run("sp", eo)


def build_nc(SEQS, L):
    S = len(SEQS)
    TT = sum(SEQS)
    NSC = TT // 128
    NCH = TT // 64
    soff = [0]
    for n in SEQS:
        soff.append(soff[-1] + n)
    nc = bass.Bass("TRN2", target_bir_lowering=False)
    _BUFS.clear()

    def din(name, shape, dt=F32):
        return nc.dram_tensor(name, shape, dt, kind="ExternalInput").ap()

    def dscr(name, shape, dt):
        return nc.dram_tensor(name, shape, dt, kind="Internal").ap()

    X = din("x", [TT, D])
    cT = din("cT", [128, 8 * S])
    w_ada = din("w_ada", [L, D, 6 * D])
    badaF = din("badaF", [L, 128, 48])
    badaR = din("badaR", [L, 2, D])
    gpreF = din("gpreF", [L, 128, 16])
    gpostR = din("gpostR", [L, 2, D])
    w_in = din("w_in", [L, D, 2592])
    wsT = din("wsT", [L, 128, 512])
    bsR = din("bsR", [L, 1, 512])
    ghF = din("ghF", [L, 128, 12])
    w_g = din("w_g", [L, 2, 16, 256])
    bgR = din("bgR", [L, 1, 512])
    w_out = din("w_out", [L, D, D])
    w_up = din("w_up", [L, D, 2 * DFF])
    w_down = din("w_down", [L, DFF, D])
    convF = din("convF", [L, 128, 176])
    consts = din("consts", [128, 640])
    Y = nc.dram_tensor("y", [TT, D], F32, kind="ExternalOutput").ap()

    XA = dscr("xa", [TT, D], F32)
    XB = dscr("xb", [TT, D], F32)
    A_s = dscr("a_s", [NSC, 128, 512], BF16)
    O_s = dscr("o_s", [NSC, 128, 512], F32)
    SR_s = dscr("sr_s", [NSC, 128, 512], BF16)
    V_s = dscr("v_s", [NSC, 128, 512], BF16)
    KD_s = dscr("kd_s", [NSC, 128, 256], BF16)
    QD_s = dscr("qd_s", [NSC, 128, 256], BF16)
    G_s = dscr("g_s", [L, 2, S, D], F32)
    WUP_s = dscr("wup_s", [L, 11, 128, 4096], BF16)

    st = ExitStack()
    ARW = 52000
    ar = st.enter_context(nc.sbuf_tensor("arena", [128, ARW], F32))
    pbt = [st.enter_context(nc.psum_tensor("pb%d" % i, [128, 512], F32)) for i in range(8)]
    PB = [Buf("pb%d" % i) for i in range(8)]
    pb = [t[:] for t in pbt]
    pbh = [t[:].bitcast(BF16) for t in pbt]

    P = Prog(nc)
    apos = [0]

    def a32(n):
        o = apos[0]
        apos[0] += n
        assert apos[0] <= ARW, ("sbuf arena overflow", apos[0])
        return ar[:, o:o + n]

    def a16(n):
        assert n % 2 == 0
        return a32(n // 2).bitcast(BF16)

    def OP(eng, method, reads, writes, *args, **kw):
        P.op(eng, lambda e: getattr(e, method)(*args, **kw), reads, writes)

    def DMA(eng, out, in_, reads, writes, sem_buf):
        P.dma(eng, lambda e: e.dma_start(out=out, in_=in_), reads, writes, sem_buf)

    def MM(out, lhsT, rhs, start, stop, reads, writes):
        P.op("pe", lambda e: e.matmul(out, lhsT=lhsT, rhs=rhs, start=start, stop=stop), reads, writes)

    def TR(out, in_, ident, reads, writes):
        P.op("pe", lambda e: e.transpose(out=out, in_=in_, identity=ident), reads, writes)

    def ACT(out, in_, func, reads, writes, **kw):
        P.op("act", lambda e: e.activation(out=out, in_=in_, func=func, **kw), reads, writes)

    cst = a32(640); Bcst = Buf("cst")
    identb = a16(128); maskU = a16(128); maskL = a16(128); ones128 = a16(128); Bcb = Buf("cb")
    triU = cst[:, 128:256]; triL = cst[:, 256:384]
    gpre = a32(L * 16); gh = a32(L * 12); cvp = a32(L * 176); badaf = a32(L * 48); Bpar = Buf("par")
    bsB = a32(L * 512); bgB = a32(L * 512)
    wsTb = a16(L * 512); Wg = a16(L * 512)
    scb = a16(8 * S); cTs = a32(8 * S); Bsc = Buf("sc")
    modA1 = [a32(8 * S) for _ in range(L)]; modB1 = [a32(8 * S) for _ in range(L)]
    modA2 = [a32(8 * S) for _ in range(L)]; modB2 = [a32(8 * S) for _ in range(L)]
    Bmod = Buf("mod")
    DECF = a32(NCH * 2); DECB = a32(NCH * 2); Bdecf = Buf("decf"); Bdecb = Buf("decb")
    neghalf = a32(1); Bnh = Buf("neghalf")
    tokt = a32(4); Btok = {e: Buf("tok" + e) for e in ("act", "dve", "pool")}
    xs = [a32(D) for _ in range(2)]; Bx = [Buf("x%d" % i) for i in range(2)]
    hn = [a16(D) for _ in range(2)]; Bhn = [Buf("hn%d" % i) for i in range(2)]
    junk = a16(D); Bjunk = Buf("junk")
    ssq = [a32(1) for _ in range(2)]; rsd = [a32(1) for _ in range(2)]
    Bss = [Buf("ss%d" % i) for i in range(2)]; Brs = [Buf("rs%d" % i) for i in range(2)]
    hT = [a16(8 * NT).rearrange("p (k t) -> p k t", k=8) for _ in range(2)]
    BhT = [[Buf("hT%d_%d" % (i, j)) for j in range(4)] for i in range(2)]
    hTh = [a16(16).rearrange("p (k t) -> p k t", k=8) for _ in range(2)]
    BhTh = [Buf("hTh%d" % i) for i in range(2)]
    W0 = a16(22 * 1024); BW0 = Buf("W0")
    W1 = a16(8 * 1024); BW1 = Buf("W1")
    wi = W0[:, 0:8 * 2624].rearrange("p (k c) -> p k c", k=8)
    wd = W0.rearrange("p (k c) -> p k c", k=22)
    wo = W1.rearrange("p (k c) -> p k c", k=8)
    Gt = a32(D); BG = Buf("G")
    tt_ = [a32(D) for _ in range(2)]; Btt = [Buf("t%d" % i) for i in range(2)]
    ss2 = [a32(1) for _ in range(2)]; ry = [a32(1) for _ in range(2)]
    Bss2 = [Buf("ss2%d" % i) for i in range(2)]; Bry = [Buf("ry%d" % i) for i in range(2)]
    PBASE = apos[0]

    toks = {
        "act": (lambda e: e.activation(out=tokt[0:1, 0:1], in_=tokt[0:1, 0:1], func=AF.Copy), Btok["act"]),
        "dve": (lambda e: e.memset(tokt[0:1, 1:2], 0.0), Btok["dve"]),
        "pool": (lambda e: e.memset(tokt[0:1, 2:3], 0.0), Btok["pool"]),
    }
    OP("dve", "memset", [], [Btok["act"]], tokt[0:1, 0:1], 0.0)
    OP("pool", "memset", [], [Bnh], neghalf, -0.5)

    def RSQ(out, in_, scale, rd, wr, np_=128, nf=1):
        OP("pool", "tensor_scalar", rd, wr, out=out, in0=in_, scalar1=scale, scalar2=EPS, op0=ALU.mult, op1=ALU.add)
        OP("pool", "tensor_tensor", wr + [Bnh], wr, out=out, in0=out, in1=neghalf[0:np_, 0:1].to_broadcast([np_, nf]), op=ALU.pow)

    BXA = Buf("XA"); BXB = Buf("XB"); BY = Buf("Y")
    Bscr = {n: Buf(n) for n in ("A", "O", "SR", "V", "KD", "QD")}
    BGs = Buf("Gs"); BWUP = Buf("WUP")

    DMA("sp", cst, consts, [], [Bcst], Bcst)
    OP("dve", "tensor_copy", [Bcst], [Bcb], out=identb, in_=cst[:, 0:128])
    OP("dve", "tensor_copy", [Bcst], [Bcb], out=maskU, in_=cst[:, 384:512])
    OP("dve", "tensor_copy", [Bcst], [Bcb], out=maskL, in_=cst[:, 512:640])
    OP("dve", "memset", [], [Bcb], ones128, 1.0 / 128)
    OP("pool", "memset", [], [Bpar], Wg, 0.0)
    for l in range(L):
        DMA("sp", gpre[:, l * 16:(l + 1) * 16], gpreF[l], [], [Bpar], Bpar)
        DMA("sp", gh[:, l * 12:(l + 1) * 12], ghF[l], [], [Bpar], Bpar)
        DMA("sp", cvp[:, l * 176:(l + 1) * 176], convF[l], [], [Bpar], Bpar)
        DMA("sp", badaf[:, l * 48:(l + 1) * 48], badaF[l], [], [Bpar], Bpar)
        DMA("sp", bsB[:, l * 512:(l + 1) * 512], bsR[l].partition_broadcast(128), [], [Bpar], Bpar)
        DMA("sp", bgB[:, l * 512:(l + 1) * 512], bgR[l].partition_broadcast(128), [], [Bpar], Bpar)
        DMA("pool", wsTb[:, l * 512:(l + 1) * 512], wsT[l], [], [Bpar], Bpar)
        DMA("pool", Wg[0:16, l * 512:l * 512 + 256], w_g[l, 0], [], [Bpar], Bpar)
        DMA("pool", Wg[32:48, l * 512 + 256:(l + 1) * 512], w_g[l, 1], [], [Bpar], Bpar)
    DMA("sp", cTs, cT, [], [Bsc], Bsc)
    ACT(scb, cTs, AF.Silu, [Bsc], [Bsc])

    pm = apos[0]
    wad = [a16(8 * 1024).rearrange("p (k c) -> p k c", k=8) for _ in range(2)]
    Bwad = [Buf("wad%d" % i) for i in range(2)]
    brow = a32(D); grow = a32(D); prow = a32(D); Brow = Buf("row"); Bgrow = [Buf("grow")]
    mraw = a32(8 * S); Bmraw = Buf("mraw")
    wstg = [a16(4096) for _ in range(2)]; Bwstg = [Buf("wstg%d" % i) for i in range(2)]
    wi_n = 0
    for l in range(L):
        wv = w_ada[l].rearrange("(k p) c -> p k c", p=128)
        for j in range(6):
            sl = wi_n % 2
            wi_n += 1
            for k0 in range(0, 8, 4):
                DMA("pool", wad[sl][:, k0:k0 + 4, :], wv[:, k0:k0 + 4, j * D:(j + 1) * D], [], [Bwad[sl]], Bwad[sl])
            if j in (0, 1, 3, 4):
                pv = pb[0][:, 0:8 * S].rearrange("p (c s) -> p c s", c=8)
                for c8 in range(8):
                    for kc in range(8):
                        MM(pv[:, c8, :], wad[sl][:, kc, c8 * 128:(c8 + 1) * 128], scb[:, kc * S:(kc + 1) * S],
                           kc == 0, kc == 7, [Bwad[sl], Bsc], [PB[0]])
                bv = badaf[:, l * 48 + j * 8: l * 48 + (j + 1) * 8].unsqueeze(2).to_broadcast([128, 8, S])
                tgt = {0: modB1, 1: modA1, 3: modB2, 4: modA2}[j][l]
                tv = tgt.rearrange("p (c s) -> p c s", c=8)
                if j in (0, 3):
                    OP("dve", "tensor_tensor", [PB[0], Bpar], [Bmod], out=tv, in0=pv, in1=bv, op=ALU.add)
                else:
                    mv = mraw.rearrange("p (c s) -> p c s", c=8)
                    OP("dve", "tensor_tensor", [PB[0], Bpar], [Bmraw], out=mv, in0=pv, in1=bv, op=ALU.add)
                    go = 0 if j == 1 else 8
                    gv = gpre[:, l * 16 + go: l * 16 + go + 8].unsqueeze(2).to_broadcast([128, 8, S])
                    OP("dve", "scalar_tensor_tensor", [Bmraw, Bpar], [Bmod], out=tv, in0=mv, scalar=1.0,
                       op0=ALU.add, in1=gv, op1=ALU.mult)
            else:
                which = 0 if j == 2 else 1
                DMA("sp", brow[0:S, :], badaR[l, which:which + 1, :].partition_broadcast(S), [], [Brow], Brow)
                DMA("sp", prow[0:S, :], gpostR[l, which:which + 1, :].partition_broadcast(S), [], [Brow], Brow)
                for half in range(2):
                    for kc in range(8):
                        MM(pb[1][0:S, :], scb[:, kc * S:(kc + 1) * S], wad[sl][:, kc, half * 512:(half + 1) * 512],
                           kc == 0, kc == 7, [Bwad[sl], Bsc], [PB[1]])
                    OP("dve", "tensor_tensor", [PB[1], Brow], Bgrow, out=grow[0:S, half * 512:(half + 1) * 512],
                       in0=pb[1][0:S, :], in1=brow[0:S, half * 512:(half + 1) * 512], op=ALU.add)
                OP("dve", "tensor_tensor", Bgrow + [Brow], Bgrow, out=grow[0:S, :], in0=grow[0:S, :], in1=prow[0:S, :], op=ALU.mult)
                DMA("sp", G_s[l, which], grow[0:S, :], Bgrow, [BGs], Bgrow[0])
        uv = w_up[l].rearrange("(k p) c -> p k c", p=128)
        for g in range(11):
            sl = g % 2
            sv = wstg[sl].rearrange("p (k a c) -> p k a c", k=8, a=2)
            DMA("pool", sv[:, :, 0, :], uv[:, :, g * 256:(g + 1) * 256], [], [Bwstg[sl]], Bwstg[sl])
            DMA("pool", sv[:, :, 1, :], uv[:, :, DFF + g * 256: DFF + (g + 1) * 256], [], [Bwstg[sl]], Bwstg[sl])
            DMA("sp", WUP_s[l, g], wstg[sl], [Bwstg[sl]], [BWUP], Bwstg[sl])
    apos[0] = pm
    P.barrier(toks)
    STOP = os.environ.get("MK_STOP", "")

    class _Stop(Exception):
        pass

    def CK(name):
        if STOP == name:
            raise _Stop()
    if STOP == "pro":
        P.emit(final_bufs=[BY])
        st.close()
        return nc

    def load_x(src, Bsrc, r0, sl):
        DMA("sp", xs[sl], src[r0:r0 + 128, :], [Bsrc], [Bx[sl]], Bx[sl])

    def norm_part1(sl, np_=128):
        ACT(junk[0:np_, :], xs[sl][0:np_, :], AF.Square, [Bx[sl]], [Bjunk, Bss[sl]], accum_out=ssq[sl][0:np_, :])
        RSQ(rsd[sl][0:np_, :], ssq[sl][0:np_, :], 1.0 / D, [Bss[sl]], [Brs[sl]], np_=np_)

    def norm_part2(sl, np_=128):
        ACT(hn[sl][0:np_, :], xs[sl][0:np_, :], AF.Copy, [Bx[sl], Brs[sl]], [Bhn[sl]], scale=rsd[sl][0:np_, :])

    def norm_to_hn(sl, np_=128):
        norm_part1(sl, np_)
        norm_part2(sl, np_)

    def prep_sub(src, Bsrc, t0, s, hsl, mA, mB, sc, sl, part):
        if part == 0:
            load_x(src, Bsrc, t0 + sc * 128, sl)
            norm_part1(sl)
        elif part == 1:
            norm_part2(sl)
        else:
            psT = pbh[0].rearrange("p (k t) -> p k t", k=8)
            for kc in range(8):
                TR(psT[:, kc, :], hn[sl][:, kc * 128:(kc + 1) * 128], identb, [Bhn[sl], Bcb], [PB[0]])
            for kc in range(8):
                OP("dve", "tensor_scalar", [PB[0], Bmod], [BhT[hsl][sc]], out=hT[hsl][:, kc, sc * 128:(sc + 1) * 128],
                   in0=psT[:, kc, :], scalar1=mA[:, kc * S + s: kc * S + s + 1], scalar2=mB[:, kc * S + s: kc * S + s + 1],
                   op0=ALU.mult, op1=ALU.add)

    def prep_tile(src, Bsrc, t0, s, hsl, mA, mB, cnt):
        psT = pbh[0].rearrange("p (k t) -> p k t", k=8)
        for sc in range(4):
            sl = cnt[0] % 2
            cnt[0] += 1
            load_x(src, Bsrc, t0 + sc * 128, sl)
            norm_to_hn(sl)
            for kc in range(8):
                TR(psT[:, kc, :], hn[sl][:, kc * 128:(kc + 1) * 128], identb, [Bhn[sl], Bcb], [PB[0]])
            for kc in range(8):
                OP("dve", "tensor_scalar", [PB[0], Bmod], [BhT[hsl][sc]], out=hT[hsl][:, kc, sc * 128:(sc + 1) * 128],
                   in0=psT[:, kc, :], scalar1=mA[:, kc * S + s: kc * S + s + 1], scalar2=mB[:, kc * S + s: kc * S + s + 1],
                   op0=ALU.mult, op1=ALU.add)

    ssh = [a32(2) for _ in range(2)]; Bssh = [Buf("ssh%d" % i) for i in range(2)]

    def post_res_a(yb, k2):
        for half in range(2):
            ACT(junk[:, half * 512:(half + 1) * 512], pb[yb[half]], AF.Square, [PB[yb[half]]], [Bjunk, Bssh[k2]],
                accum_out=ssh[k2][:, half:half + 1])
        OP("pool", "tensor_tensor", [Bssh[k2]], [Bss2[k2]], out=ss2[k2], in0=ssh[k2][:, 0:1], in1=ssh[k2][:, 1:2], op=ALU.add)
        RSQ(ry[k2], ss2[k2], 1.0 / D, [Bss2[k2]], [Bry[k2]])

    def post_res_b(yb, xap, Bxb, k2, dst, Bdst, r0):
        for half in range(2):
            OP("dve", "scalar_tensor_tensor", [PB[yb[half]], Bry[k2], BG], [Btt[k2]],
               out=tt_[k2][:, half * 512:(half + 1) * 512], in0=pb[yb[half]], scalar=ry[k2], op0=ALU.mult,
               in1=Gt[:, half * 512:(half + 1) * 512], op1=ALU.mult)
        OP("pool", "tensor_tensor", [Btt[k2], Bxb], [Btt[k2]], out=tt_[k2], in0=tt_[k2], in1=xap, op=ALU.add)
        DMA("pool", dst[r0:r0 + 128, :], tt_[k2], [Btt[k2]], [Bdst], Btt[k2])

    def post_res(yb, xap, Bxb, k2, dst, Bdst, r0):
        post_res_a(yb, k2)
        post_res_b(yb, xap, Bxb, k2, dst, Bdst, r0)

    def load_G(l, which, s):
        DMA("sp", Gt, G_s[l, which, s:s + 1, :].partition_broadcast(128), [BGs], [BG], BG)

    tiles = []
    for s in range(S):
        for ti in range(SEQS[s] // NT):
            tiles.append((s, ti, soff[s] + ti * NT))

    for l in range(L):
        XIN, BXIN = (X, Buf("Xin")) if l == 0 else (XB, BXB)
        XOUT3, BXOUT3 = (Y, BY) if l == L - 1 else (XB, BXB)
        lp = l * 512
        p1 = apos[0]
        wv = w_in[l].rearrange("(k p) c -> p k c", p=128)
        OP("pool", "memset", [], [BW0], wi[:, :, 1536:1600], 0.0)
        for (d0, s0, n) in ((0, 0, 512), (512, 1024, 256), (768, 1280, 256), (1024, 2048, 512),
                            (1536, 2560, 16), (1568, 2576, 16), (1600, 512, 512), (2112, 1536, 512)):
            DMA("pool", wi[:, :, d0:d0 + n], wv[:, :, s0:s0 + n], [], [BW0], BW0)
        ov = w_out[l].rearrange("(k p) c -> p k c", p=128)
        for k0 in range(0, 8, 4):
            DMA("pool", wo[:, k0:k0 + 4, :], ov[:, k0:k0 + 4, :], [], [BW1], BW1)

        uTs = [a32(4 * NT).rearrange("p (k t) -> p k t", k=4) for _ in range(2)]; BuTs = [Buf("uT0"), Buf("uT1")]
        qT = a32(2 * NT).rearrange("p (k t) -> p k t", k=2); kT = a32(2 * NT).rearrange("p (k t) -> p k t", k=2)
        BqT = Buf("qT"); BkT = Buf("kT")
        srt = a16(4 * NT).rearrange("p (k t) -> p k t", k=4); Bsr = Buf("sr")
        lT = a16(NT); BlT = Buf("lT")
        vbf = [a16(512) for _ in range(3)]; Bvbf = [Buf("vbf%d" % i) for i in range(3)]
        zt = a32(512); Bzt = Buf("zt")
        spt = a32(512); Bsp = Buf("sp")
        Ep = a32(512).rearrange("p (a t) -> p a t", a=4); Em = a32(512).rearrange("p (a t) -> p a t", a=4)
        BEp = Buf("Ep"); BEm = Buf("Em")
        qd = [a16(512).rearrange("p (a t) -> p a t", a=4) for _ in range(3)]; Bqd = [Buf("qd%d" % i) for i in range(3)]
        kds = [a16(512).rearrange("p (a t) -> p a t", a=4) for _ in range(2)]; Bkds = [Buf("kd%d" % i) for i in range(2)]
        kdt = [a16(512).rearrange("p (a t) -> p a t", a=4) for _ in range(3)]; Bkdt = [Buf("kdt%d" % i) for i in range(3)]
        scms = [[a16(512).rearrange("p (a t) -> p a t", a=4) for _ in range(2)] for _ in range(2)]
        Bscms = [[Buf("scm%d_%d" % (i, j)) for j in range(2)] for i in range(2)]
        Pst = a32(512); BPst = Buf("Pst")
        Sbf = [a16(512) for _ in range(2)]; BSbf = [Buf("Sbf0"), Buf("Sbf1")]
        vsq = a32(512); Bvsq = Buf("vsq")
        vsb = a32(512); Bvsb = Buf("vsb")
        st1 = a32(4); st2 = a32(4); mean = a32(4); msq = a32(4); var = a32(4); rsv = a32(4)
        Bst = Buf("st")
        vhats = [a16(512) for _ in range(2)]; Bvhats = [Buf("vhat0"), Buf("vhat1")]
        tmx = a32(512); Btmx = Buf("tmx")
        apre = a32(512); Bapre = Buf("apre")
        asq = a16(512); Basq = Buf("asq")
        ra = a32(512); Bra = Buf("ra")
        aTt = [a16(512) for _ in range(2)]; BaT = [Buf("aT%d" % i) for i in range(2)]
        oloc = [a32(512) for _ in range(2)]; Boloc = [Buf("oloc%d" % i) for i in range(2)]

        cnt = [0]
        hsl_n = [0]

        def p1_prep(idx):
            s, ti, t0 = tiles[idx]
            prep_tile(XIN, BXIN, t0, s, idx % 2, modA1[l], modB1[l], cnt)

        def p1_phaseA(idx):
            hsl = idx % 2
            h = hT[hsl]
            rd = BhT[hsl] + [BW0]
            chunks = [("u", i, i * 128, 128) for i in range(4)] + [("q", i, 512 + i * 128, 128) for i in range(2)] + \
                     [("k", i, 768 + i * 128, 128) for i in range(2)] + [("r", i, 1024 + i * 128, 128) for i in range(4)] + \
                     [("l", 0, 1536, 64)]
            for ci, (kind, i, c0, m) in enumerate(chunks):
                b = 1 + ci % 2
                for kc in range(8):
                    MM(pb[b][0:m, :], wi[:, kc, c0:c0 + m], h[:, kc, :], kc == 0, kc == 7, rd, [PB[b]])
                if kind == "u":
                    ACT(uTs[hsl][:, i, :], pb[b], AF.Copy, [PB[b]], [BuTs[hsl]])
                elif kind == "q":
                    ACT(qT[:, i, :], pb[b], AF.Copy, [PB[b]], [BqT], scale=0.125)
                elif kind == "k":
                    ACT(kT[:, i, :], pb[b], AF.Copy, [PB[b]], [BkT])
                elif kind == "r":
                    ACT(srt[:, i, :], pb[b], AF.Silu, [PB[b]], [Bsr])
                else:
                    OP("dve", "tensor_copy", [PB[b]], [BlT], out=lT[0:64, :], in_=pb[b][0:64, :])

        subs = []
        for idx in range(len(tiles)):
            s_, ti_, t0_ = tiles[idx]
            for sc in range(4):
                subs.append((idx, sc, ti_ == 0 and sc == 0))
        NSUB = len(subs)

        def p1_A(n):
            idx, sc, seq_first = subs[n]
            s, ti, t0 = tiles[idx]
            hsl = idx % 2
            h = hT[hsl]
            gsc = (t0 + sc * 128) // 128
            k3 = n % 3
            k2 = n % 2
            cs = slice(sc * 128, (sc + 1) * 128)
            kd = kds[k2]; Bkd = Bkds[k2]
            vhat = vhats[k2]; Bvhat = Bvhats[k2]
            for kc in range(8):
                MM(pb[4], h[:, kc, cs], wi[:, kc, 2112:2624], kc == 0, kc == 7, [BhT[hsl][sc], BW0], [PB[4]])
            MM(pb[5], lT[0:48, cs], Wg[0:48, lp:lp + 512], True, True, [BlT, Bpar], [PB[5]])
            for kc in range(8):
                MM(pb[3], h[:, kc, cs], wi[:, kc, 1600:2112], kc == 0, kc == 7, [BhT[hsl][sc], BW0], [PB[3]])
            ACT(vbf[k3], pb[4], AF.Copy, [PB[4]], [Bvbf[k3]])
            DMA("sp", V_s[gsc], vbf[k3], [Bvbf[k3]], [Bscr["V"]], Bvbf[k3])
            OP("dve", "tensor_tensor", [PB[5], Bpar], [Bzt], out=zt, in0=pb[5], in1=bgB[:, lp:lp + 512], op=ALU.add)
            ACT(spt, zt, AF.Exp, [Bzt], [Bsp], scale=-1.0)
            ACT(spt, spt, AF.Ln, [Bsp], [Bsp], bias=1.0)
            OMIT = os.environ.get("MK_OMIT", "")
            ACT(vsb, pb[3], AF.Copy, [PB[3]], [Bvsb])
            v3 = vsb.rearrange("p (h t) -> p h t", h=4)
            if "s" not in OMIT:
                OP("dve", "tensor_reduce", [Bvsb], [Bst], out=st1, in_=v3, axis=AX.X, op=ALU.add)
                ACT(vsq, vsb, AF.Square, [Bvsb], [Bvsq])
                OP("dve", "tensor_reduce", [Bvsq], [Bst], out=st2, in_=vsq.rearrange("p (h t) -> p h t", h=4), axis=AX.X, op=ALU.add)
            if "s" not in OMIT:
                OP("dve", "tensor_scalar", [Bst], [Bst], out=mean, in0=st1, scalar1=1.0 / 128, scalar2=None, op0=ALU.mult)
                OP("dve", "tensor_tensor", [Bst], [Bst], out=msq, in0=mean, in1=mean, op=ALU.mult)
                OP("dve", "scalar_tensor_tensor", [Bst], [Bst], out=var, in0=st2, scalar=1.0 / 128, op0=ALU.mult, in1=msq, op1=ALU.subtract)
                RSQ(rsv, var, 1.0, [Bst], [Bst], nf=4)
            yield
            bT = pb[6].rearrange("p (a t) -> p a t", a=4)
            for dr in range(2):
                for dc in range(2):
                    a = dr * 2 + dc
                    if "f" in OMIT:
                        continue
                    MM(bT[:, a, :], spt[:, dr * 256 + dc * 128: dr * 256 + (dc + 1) * 128], triU if dr == 0 else triL,
                       True, True, [Bsp, Bcst], [PB[6]])
            ACT(Ep, bT, AF.Exp, [PB[6]], [BEp])
            ACT(Em, bT, AF.Exp, [PB[6]], [BEm], scale=-1.0)
            for hh in range(4 if "s" not in OMIT else 0):
                OP("dve", "tensor_scalar", [Bvsb, Bst], [Bvhat], out=vhat[:, hh * 128:(hh + 1) * 128], in0=v3[:, hh, :],
                   scalar1=mean[:, hh:hh + 1], scalar2=rsv[:, hh:hh + 1], op0=ALU.subtract, op1=ALU.mult)
            for c in range(2):
                gc = gsc * 2 + c
                OP("pool", "tensor_copy", [BEp], [Bdecf], out=DECF[:, gc * 2:gc * 2 + 2], in_=Ep[:, 0:2, c * 64 + 63])
                OP("pool", "tensor_copy", [BEp], [Bdecb], out=DECB[:, gc * 2:gc * 2 + 2], in_=Ep[:, 2:4, c * 64])
            qv = qT[:, :, cs].unsqueeze(1).to_broadcast([128, 2, 2, 128])
            kv = kT[:, :, cs].unsqueeze(1).to_broadcast([128, 2, 2, 128])
            OP("pool", "tensor_tensor", [BkT, BEm], [Bkd], out=kd.rearrange("p (r c) t -> p r c t", r=2),
               in0=kv, in1=Em.rearrange("p (r c) t -> p r c t", r=2), op=ALU.mult)
            OP("pool", "tensor_tensor", [BqT, BEp], [Bqd[k3]], out=qd[k3].rearrange("p (r c) t -> p r c t", r=2),
               in0=qv, in1=Ep.rearrange("p (r c) t -> p r c t", r=2), op=ALU.mult)
            DMA("sp", QD_s[gsc].rearrange("p (a t) -> p a t", a=2), qd[k3][:, 2:4, :], [Bqd[k3]], [Bscr["QD"]], Bqd[k3])
            yield
            kdT = pbh[7][:, 0:512].rearrange("p (a t) -> p a t", a=4)
            if "k" not in OMIT:
                for a in range(4):
                    TR(kdT[:, a, :], kd[:, a, :], identb, [Bkd, Bcb], [PB[7]])
                ACT(kdt[k3], kdT, AF.Copy, [PB[7]], [Bkdt[k3]])
                DMA("sp", KD_s[gsc].rearrange("p (a t) -> p a t", a=2), kdt[k3][:, 2:4, :], [Bkdt[k3]], [Bscr["KD"]], Bkdt[k3])
            DMA("sp", SR_s[gsc].rearrange("p (h t) -> p h t", h=4), srt[:, :, cs], [Bsr], [Bscr["SR"]], Bsr)

        def p1_B(n):
            idx, sc, seq_first = subs[n]
            s, ti, t0 = tiles[idx]
            hsl = idx % 2
            gsc = (t0 + sc * 128) // 128
            k3 = n % 3
            k2 = n % 2
            cs = slice(sc * 128, (sc + 1) * 128)
            kd = kds[k2]; Bkd = Bkds[k2]
            vhat = vhats[k2]; Bvhat = Bvhats[k2]
            scm = scms[k2]; Bscm = Bscms[k2]
            uT = uTs[hsl]; BuT = BuTs[hsl]
            mx = pb[3].rearrange("p (h t) -> p h t", h=4)
            for hh in range(4):
                MM(mx[:, hh, :], vhat[:, hh * 128:(hh + 1) * 128], wsTb[:, lp + hh * 128: lp + (hh + 1) * 128], True, True,
                   [Bvhat, Bpar], [PB[3]])
            for dr in range(2):
                for hh in range(4):
                    a = dr * 2 + hh // 2
                    ps_ = slice((hh % 2) * 64, (hh % 2) * 64 + 64)
                    sb_ = 1 + hh % 2
                    sv = pb[sb_].rearrange("p (h t) -> p h t", h=4)
                    MM(sv[:, dr * 2 + hh // 2, :], kd[ps_, a, :], qd[k3][ps_, a, :], True, True, [Bkd, Bqd[k3]], [PB[sb_]])
            for hh in range(4):
                OP("dve", "scalar_tensor_tensor", [PB[3], Bpar], [Btmx], out=tmx[:, hh * 128:(hh + 1) * 128], in0=mx[:, hh, :],
                   scalar=gh[:, l * 12 + hh: l * 12 + hh + 1], op0=ALU.mult, in1=bsB[:, lp + hh * 128: lp + (hh + 1) * 128], op1=ALU.add)
            OP("pool", "tensor_tensor", [Btmx, BuT], [Bapre], out=apre.rearrange("p (h t) -> p h t", h=4),
               in0=tmx.rearrange("p (h t) -> p h t", h=4), in1=uT[:, :, cs], op=ALU.mult)
            for par in range(2):
                sv = pb[1 + par].rearrange("p (h t) -> p h t", h=4)
                for dr in range(2):
                    OP("dve", "tensor_tensor", [PB[1 + par], Bcb], [Bscm[dr]],
                       out=scm[dr].rearrange("p (g e) t -> p g e t", e=2)[:, :, par, :], in0=sv[:, dr * 2:dr * 2 + 2, :],
                       in1=(maskU if dr == 0 else maskL).unsqueeze(1).to_broadcast([128, 2, 128]), op=ALU.mult)
            yield
            ACT(asq, apre, AF.Square, [Bapre], [Basq])
            yield
            MM(pb[7], ones128, asq, True, True, [Basq, Bcb], [PB[7]])
            ACT(ra, pb[7], AF.Ln, [PB[7]], [Bra], bias=EPS)
            ACT(ra, ra, AF.Exp, [Bra], [Bra], scale=-0.5)
            for hh in range(4):
                OP("dve", "scalar_tensor_tensor", [Bapre, Bpar, Bra], [BaT[k2]], out=aTt[k2][:, hh * 128:(hh + 1) * 128],
                   in0=apre[:, hh * 128:(hh + 1) * 128], scalar=gh[:, l * 12 + 4 + hh: l * 12 + 5 + hh], op0=ALU.mult,
                   in1=ra[:, hh * 128:(hh + 1) * 128], op1=ALU.mult)
            DMA("sp", A_s[gsc], aTt[k2], [BaT[k2]], [Bscr["A"]], BaT[k2])

        def p1_C(n):
            idx, sc, seq_first = subs[n]
            s, ti, t0 = tiles[idx]
            gsc = (t0 + sc * 128) // 128
            k3 = n % 3
            k2 = n % 2
            scm = scms[k2]; Bscm = Bscms[k2]
            if seq_first:
                OP("dve", "memset", [], [BPst], Pst, 0.0)
                OP("dve", "memset", [], [BSbf[0]], Sbf[0], 0.0)
                OP("dve", "memset", [], [BSbf[1]], Sbf[1], 0.0)
            ovb = (pb[4][:, 0:256].rearrange("p (h t) -> p h t", h=2), pb[3][:, 0:256].rearrange("p (h t) -> p h t", h=2))
            obk = (4, 3)
            dsb = (5, 6)
            for c in range(2):
                db = dsb[c]
                for hp in range(2):
                    MM(pb[db][:, hp * 256:(hp + 1) * 256], kdt[k3][c * 64:(c + 1) * 64, hp, :],
                       vbf[k3][c * 64:(c + 1) * 64, hp * 256:(hp + 1) * 256], True, True, [Bkdt[k3], Bvbf[k3]], [PB[db]])
            for c in range(2):
                if c == 1:
                    yield
                gc = gsc * 2 + c
                first = seq_first and c == 0
                if c == 0:
                    for hh in range(4):
                        ovh = ovb[hh % 2][:, hh // 2, :]
                        MM(ovh, vbf[k3][:, hh * 128:(hh + 1) * 128], scm[0][:, hh, :], hh < 2, False,
                           [Bvbf[k3], Bscm[0]], [PB[obk[hh % 2]]])
                        MM(ovh, vbf[k3][:, hh * 128:(hh + 1) * 128], scm[1][:, hh, :], False, False,
                           [Bvbf[k3], Bscm[1]], [PB[obk[hh % 2]]])
                sidx = gc % 2
                for hh in range(4):
                    ps_ = slice((hh % 2) * 64, (hh % 2) * 64 + 64)
                    col = (hh // 2) * 256 + (hh % 2) * 128
                    MM(ovb[hh % 2][:, hh // 2, c * 64:(c + 1) * 64], Sbf[sidx][ps_, col:col + 128],
                       qd[k3][ps_, hh // 2, c * 64:(c + 1) * 64],
                       False, c == 1 and hh >= 2, [BSbf[sidx], Bqd[k3]], [PB[obk[hh % 2]]])
                db = dsb[c]
                for hp in range(2):
                    dprev = DECF[:, (gc - 1) * 2 + hp:(gc - 1) * 2 + hp + 1] if not first else DECF[:, gc * 2 + hp: gc * 2 + hp + 1]
                    dcur = DECF[:, gc * 2 + hp: gc * 2 + hp + 1]
                    OP("dve", "scalar_tensor_tensor", [BPst, Bdecf, PB[db]], [BPst], out=Pst[:, hp * 256:(hp + 1) * 256],
                       in0=Pst[:, hp * 256:(hp + 1) * 256], scalar=dprev, op0=ALU.mult, in1=pb[db][:, hp * 256:(hp + 1) * 256], op1=ALU.add)
                    OP("dve", "tensor_scalar", [BPst, Bdecf], [BSbf[1 - sidx]], out=Sbf[1 - sidx][:, hp * 256:(hp + 1) * 256],
                       in0=Pst[:, hp * 256:(hp + 1) * 256], scalar1=dcur, scalar2=None, op0=ALU.mult)
            for par in range(2):
                ACT(oloc[k2].rearrange("p (g e t) -> p g e t", e=2, t=128)[:, :, par, :], ovb[par], AF.Copy,
                    [PB[obk[par]]], [Boloc[k2]])
            DMA("sp", O_s[gsc], oloc[k2], [Boloc[k2]], [Bscr["O"]], Boloc[k2])

        try:
            p1_prep(0)
            def adv(g):
                if g is not None:
                    try:
                        next(g)
                    except StopIteration:
                        pass

            p1slot = {}

            def p1_prep_part(t, part):
                if t >= NSUB:
                    return
                idx, sc, _sf = subs[t]
                if idx + 1 >= len(tiles):
                    return
                s_, ti_, t0_ = tiles[idx + 1]
                if part == 0:
                    p1slot[(idx + 1, sc)] = cnt[0] % 2
                    cnt[0] += 1
                prep_sub(XIN, BXIN, t0_, s_, (idx + 1) % 2, modA1[l], modB1[l], sc, p1slot[(idx + 1, sc)], part)

            for t in range(NSUB + 2):
                gB = p1_B(t - 1) if 0 <= t - 1 < NSUB else None
                gA = p1_A(t) if t < NSUB else None
                p1_prep_part(t, 0)
                adv(gB)
                if t < NSUB:
                    idx, sc, _sf = subs[t]
                    if sc == 0:
                        p1_phaseA(idx)
                adv(gA)
                p1_prep_part(t, 1)
                adv(gB)
                gC = p1_C(t - 2) if 0 <= t - 2 < NSUB else None
                adv(gC)
                adv(gB)
                p1_prep_part(t, 2)
                adv(gC)
                adv(gA)
                adv(gA)
        except _Stop:
            P.barrier(toks)
            break
        apos[0] = p1
        P.barrier(toks)
        if STOP == "p1":
            break

        dv = w_down[l].rearrange("(k p) c -> p k c", p=128)
        for k0 in range(0, 22, 2):
            DMA("pool", wd[:, k0:k0 + 2, :], dv[:, k0:k0 + 2, :], [], [BW0], BW0)
        NS2 = 4
        x2 = [a32(D) for _ in range(NS2)]; Bx2 = [Buf("x2_%d" % i) for i in range(NS2)]
        aL = [a16(512) for _ in range(NS2)]; BaL = [Buf("aL%d" % i) for i in range(NS2)]
        oL = [a32(512) for _ in range(NS2)]; BoL = [Buf("oL%d" % i) for i in range(NS2)]
        srL = [a16(512) for _ in range(NS2)]; BsrL = [Buf("srL%d" % i) for i in range(NS2)]
        vL = [a16(512) for _ in range(NS2)]; BvL = [Buf("vL%d" % i) for i in range(NS2)]
        kdL = [a16(256).rearrange("p (a t) -> p a t", a=2) for _ in range(NS2)]; BkdL = [Buf("kdL%d" % i) for i in range(NS2)]
        qdL = [a16(256).rearrange("p (a t) -> p a t", a=2) for _ in range(NS2)]; BqdL = [Buf("qdL%d" % i) for i in range(NS2)]
        Pb = a32(512); BPb = Buf("Pb")
        Sb = [a16(512) for _ in range(2)]; BSb = [Buf("Sb0"), Buf("Sb1")]
        osum = [a32(512) for _ in range(2)]; Bosum = [Buf("osum0"), Buf("osum1")]
        osq = a16(512); Bosq = Buf("osq")
        ro = a32(512); Bro = Buf("ro")
        on = a32(512); Bon = Buf("on")
        oTn = [a16(512) for _ in range(2)]; BoTn = [Buf("oTn0"), Buf("oTn1")]

        order = []
        for s in range(S):
            r0 = soff[s]
            nsc = SEQS[s] // 128
            for j in range(nsc - 1, -1, -1):
                order.append((s, r0 // 128 + j, j == nsc - 1))
        N2 = len(order)

        def p2_load(n):
            s, gsc, lastsc = order[n]
            k = n % NS2
            DMA("sp", x2[k], XIN[gsc * 128:(gsc + 1) * 128, :], [BXIN], [Bx2[k]], Bx2[k])
            DMA("sp", aL[k], A_s[gsc], [Bscr["A"]], [BaL[k]], BaL[k])
            DMA("sp", oL[k], O_s[gsc], [Bscr["O"]], [BoL[k]], BoL[k])
            DMA("sp", srL[k], SR_s[gsc], [Bscr["SR"]], [BsrL[k]], BsrL[k])
            DMA("sp", vL[k], V_s[gsc], [Bscr["V"]], [BvL[k]], BvL[k])
            DMA("sp", kdL[k], KD_s[gsc].rearrange("p (a t) -> p a t", a=2), [Bscr["KD"]], [BkdL[k]], BkdL[k])
            DMA("sp", qdL[k], QD_s[gsc].rearrange("p (a t) -> p a t", a=2), [Bscr["QD"]], [BqdL[k]], BqdL[k])

        ovb2 = (pb[0][:, 0:256].rearrange("p (h t) -> p h t", h=2), pb[3][:, 0:256].rearrange("p (h t) -> p h t", h=2))
        obk2 = (0, 3)

        def p2_A(n, part):
            s, gsc, lastsc = order[n]
            k = n % NS2
            ko = n % 2
            if part == 0:
                if lastsc:
                    OP("dve", "memset", [], [BPb], Pb, 0.0)
                    OP("dve", "memset", [], [BSb[0]], Sb[0], 0.0)
                    OP("dve", "memset", [], [BSb[1]], Sb[1], 0.0)
                for c in (1, 0):
                    db = 1 + c
                    for hp in range(2):
                        MM(pb[db][:, hp * 256:(hp + 1) * 256], kdL[k][c * 64:(c + 1) * 64, hp, :],
                           vL[k][c * 64:(c + 1) * 64, hp * 256:(hp + 1) * 256], True, True, [BkdL[k], BvL[k]], [PB[db]])
            c = 1 if part == 0 else 0
            gc = gsc * 2 + c
            first = lastsc and c == 1
            sidx = gc % 2
            db = 1 + c
            for hh in range(4):
                ps_ = slice((hh % 2) * 64, (hh % 2) * 64 + 64)
                col = (hh // 2) * 256 + (hh % 2) * 128
                MM(ovb2[hh % 2][:, hh // 2, c * 64:(c + 1) * 64], Sb[sidx][ps_, col:col + 128],
                   qdL[k][ps_, hh // 2, c * 64:(c + 1) * 64],
                   True, True, [BSb[sidx], BqdL[k]], [PB[obk2[hh % 2]]])
            for hp in range(2):
                dprev = DECB[:, (gc + 1) * 2 + hp:(gc + 1) * 2 + hp + 1] if not first else DECB[:, gc * 2 + hp: gc * 2 + hp + 1]
                dcur = DECB[:, gc * 2 + hp: gc * 2 + hp + 1]
                OP("dve", "scalar_tensor_tensor", [BPb, Bdecb, PB[db]], [BPb], out=Pb[:, hp * 256:(hp + 1) * 256],
                   in0=Pb[:, hp * 256:(hp + 1) * 256], scalar=dprev, op0=ALU.mult, in1=pb[db][:, hp * 256:(hp + 1) * 256], op1=ALU.add)
                OP("dve", "tensor_scalar", [BPb, Bdecb], [BSb[1 - sidx]], out=Sb[1 - sidx][:, hp * 256:(hp + 1) * 256],
                   in0=Pb[:, hp * 256:(hp + 1) * 256], scalar1=dcur, scalar2=None, op0=ALU.mult)
            if part == 1:
                for par in range(2):
                    OP("dve", "tensor_tensor", [PB[obk2[par]], BoL[k]], [Bosum[ko]],
                       out=osum[ko].rearrange("p (g e t) -> p g e t", e=2, t=128)[:, :, par, :], in0=ovb2[par],
                       in1=oL[k].rearrange("p (g e t) -> p g e t", e=2, t=128)[:, :, par, :], op=ALU.add)

        def p2_B(n):
            s, gsc, lastsc = order[n]
            k = n % NS2
            ko = n % 2
            ACT(osq, osum[ko], AF.Square, [Bosum[ko]], [Bosq])
            MM(pb[6], ones128, osq, True, True, [Bosq, Bcb], [PB[6]])
            ACT(ro, pb[6], AF.Ln, [PB[6]], [Bro], bias=EPS)
            ACT(ro, ro, AF.Exp, [Bro], [Bro], scale=-0.5)
            yield
            for hh in range(4):
                OP("dve", "scalar_tensor_tensor", [Bosum[ko], Bpar, Bro], [Bon], out=on[:, hh * 128:(hh + 1) * 128],
                   in0=osum[ko][:, hh * 128:(hh + 1) * 128], scalar=gh[:, l * 12 + 8 + hh: l * 12 + 9 + hh], op0=ALU.mult,
                   in1=ro[:, hh * 128:(hh + 1) * 128], op1=ALU.mult)
            OP("pool", "tensor_tensor", [Bon, BsrL[k]], [BoTn[ko]], out=oTn[ko], in0=on, in1=srL[k], op=ALU.mult)

        def p2_C(n, part):
            s, gsc, lastsc = order[n]
            k = n % NS2
            ko = n % 2
            yb = (4, 5)
            if part == 0 and lastsc:
                load_G(l, 0, s)
            half = part
            for kc in range(8):
                lhs = aL[k][:, kc * 128:(kc + 1) * 128] if kc < 4 else oTn[ko][:, (kc - 4) * 128:(kc - 3) * 128]
                MM(pb[yb[half]], lhs, wo[:, kc, half * 512:(half + 1) * 512], kc == 0, kc == 7,
                   [BaL[k], BoTn[ko], BW1], [PB[yb[half]]])
            if part == 1:
                post_res_a(yb, ko)

        def p2_Cpost(n):
            s, gsc, lastsc = order[n]
            post_res_b((4, 5), x2[n % NS2], Bx2[n % NS2], n % 2, XA, BXA, gsc * 128)

        p2_load(0)
        if N2 > 1:
            p2_load(1)
        def adv2(g):
            if g is not None:
                try:
                    next(g)
                except StopIteration:
                    pass

        for t in range(N2 + 2):
            gB2 = p2_B(t - 1) if 0 <= t - 1 < N2 else None
            adv2(gB2)
            if t < N2:
                p2_A(t, 0)
            if 0 <= t - 2 < N2:
                p2_C(t - 2, 0)
            if t < N2:
                p2_A(t, 1)
            if 0 <= t - 2 < N2:
                p2_C(t - 2, 1)
            adv2(gB2)
            if 0 <= t - 2 < N2:
                p2_Cpost(t - 2)
            if t + 2 < N2:
                p2_load(t + 2)
        apos[0] = p1
        P.barrier(toks)
        if STOP == "p2":
            break

        act = a16(22 * NT).rearrange("p (k t) -> p k t", k=22); Bact = [Buf("act%d" % i) for i in range(4)]
        wup = [a16(4096).rearrange("p (k a c) -> p k a c", k=8, a=2) for _ in range(3)]; Bwup = [Buf("wup%d" % i) for i in range(3)]
        cg = [a32(NT) for _ in range(2)]; cv = [a32(NT) for _ in range(2)]; sg = [a32(NT) for _ in range(2)]
        Bcg = [Buf("cg%d" % i) for i in range(2)]; Bcv = [Buf("cv%d" % i) for i in range(2)]; Bsg = [Buf("sg%d" % i) for i in range(2)]
        xh = a32(D); Bxh = Buf("xh")
        hsb = [a32(2) for _ in range(4)]; Bhsb = [Buf("hsb%d" % i) for i in range(4)]
        cnt3 = [0]
        gcount = [0]
        upn = [0]
        pn = [0]

        hnh = a16(D); Bhnh = Buf("hnh")
        ssqh = a32(1); rsdh = a32(1); Bssh_ = Buf("ssqh"); Brsh = Buf("rsdh")
        OP("pool", "memset", [], [Bxh], xh[0:2, :], 0.0)
        p3slot = {}

        def p3_prep_step(idx, g):
            s, ti, t0 = tiles[idx]
            hsl = idx % 2
            hh_ = hTh[hsl]
            lo_ok = ti > 0
            hi_ok = (ti + 1) * NT < SEQS[s]
            if g == 0:
                if lo_ok:
                    DMA("sp", xh[0:1, :], XA[t0 - 1:t0, :], [BXA], [Bxh], Bxh)
                if hi_ok:
                    DMA("sp", xh[1:2, :], XA[t0 + NT:t0 + NT + 1, :], [BXA], [Bxh], Bxh)
                ACT(junk[0:2, :], xh[0:2, :], AF.Square, [Bxh], [Bjunk, Bssh_], accum_out=ssqh[0:2, :])
                RSQ(rsdh[0:2, :], ssqh[0:2, :], 1.0 / D, [Bssh_], [Brsh], np_=2)
            if g == 1:
                ACT(hnh[0:2, :], xh[0:2, :], AF.Copy, [Bxh, Brsh], [Bhnh], scale=rsdh[0:2, :])
            if g == 2:
                psH = pbh[0][:, 0:16]
                for kc in range(8):
                    TR(psH[:, 2 * kc:2 * kc + 2], hnh[0:2, kc * 128:(kc + 1) * 128], identb[0:2, 0:2], [Bhnh, Bcb], [PB[0]])
                for kc in range(8):
                    OP("dve", "tensor_scalar", [PB[0], Bmod], [BhTh[hsl]], out=hh_[:, kc, :],
                       in0=psH[:, 2 * kc:2 * kc + 2], scalar1=modA2[l][:, kc * S + s: kc * S + s + 1],
                       scalar2=modB2[l][:, kc * S + s: kc * S + s + 1], op0=ALU.mult, op1=ALU.add)
                if not lo_ok:
                    OP("dve", "memset", [], [BhTh[hsl]], hh_[:, :, 0:1], 0.0)
                if not hi_ok:
                    OP("dve", "memset", [], [BhTh[hsl]], hh_[:, :, 1:2], 0.0)
            if g in (1, 3, 5, 7):
                sc = (g - 1) // 2
                sl = cnt3[0] % 2
                cnt3[0] += 1
                p3slot[(idx, sc)] = sl
                prep_sub(XA, BXA, t0, s, hsl, modA2[l], modB2[l], sc, sl, 0)
            if g in (2, 4, 6, 8):
                sc = (g - 2) // 2
                prep_sub(XA, BXA, t0, s, hsl, modA2[l], modB2[l], sc, p3slot[(idx, sc)], 1)
            if g in (3, 5, 7, 9):
                sc = (g - 3) // 2
                prep_sub(XA, BXA, t0, s, hsl, modA2[l], modB2[l], sc, p3slot[(idx, sc)], 2)

        def p3_prep(idx):
            for g in range(10):
                p3_prep_step(idx, g)

        def p3_loadw(gi):
            g = gi % 11
            sl = gi % 3
            DMA("sp", wup[sl].rearrange("p k a c -> p (k a c)"), WUP_s[l, g], [BWUP], [Bwup[sl]], Bwup[sl])

        ngroups = 11 * len(tiles)

        def p3_main(idx):
            s, ti, t0 = tiles[idx]
            hsl = idx % 2
            h = hT[hsl]
            hh_ = hTh[hsl]
            hvs = [pb[4 + i][:, 0:176].rearrange("p (m c) -> p m c", m=44) for i in range(2)]
            if ti == 0:
                load_G(l, 1, s)
            for g in range(11):
                gi = idx * 11 + g
                if gi + 2 < ngroups:
                    p3_loadw(gi + 2)
                sl = gi % 3
                for pi in range(2):
                    m = g * 2 + pi
                    k2 = pn[0] % 2
                    pn[0] += 1
                    res = []
                    for a in range(2):
                        b = 1 + upn[0] % 3
                        hb = 4 + upn[0] % 2
                        hv = hvs[upn[0] % 2]
                        upn[0] += 1
                        mm_ = m + 22 * a
                        for kc in range(8):
                            MM(pb[b], wup[sl][:, kc, a, pi * 128:(pi + 1) * 128], h[:, kc, :], kc == 0, kc == 7,
                               BhT[hsl] + [Bwup[sl]], [PB[b]])
                        for kc in range(8):
                            MM(hv[:, mm_, 0:2], wup[sl][:, kc, a, pi * 128:(pi + 1) * 128], hh_[:, kc, :], kc == 0, kc == 7,
                               [BhTh[hsl], Bwup[sl]], [PB[hb]])
                        dst, Bd = (cg[k2], Bcg[k2]) if a == 0 else (cv[k2], Bcv[k2])
                        cp = cvp[:, l * 176 + mm_ * 4: l * 176 + mm_ * 4 + 4]
                        ACT(dst, pb[b], AF.Identity, [PB[b], Bpar], [Bd], scale=cp[:, 1:2], bias=cp[:, 3:4])
                        OP("dve", "scalar_tensor_tensor", [PB[b], Bpar, Bd], [Bd], out=dst[:, 1:NT], in0=pb[b][:, 0:NT - 1],
                           scalar=cp[:, 0:1], op0=ALU.mult, in1=dst[:, 1:NT], op1=ALU.add)
                        OP("dve", "scalar_tensor_tensor", [PB[b], Bpar, Bd], [Bd], out=dst[:, 0:NT - 1], in0=pb[b][:, 1:NT],
                           scalar=cp[:, 2:3], op0=ALU.mult, in1=dst[:, 0:NT - 1], op1=ALU.add)
                        hsl_ = (upn[0] - 1) % 4
                        ACT(hsb[hsl_], hv[:, mm_, 0:2], AF.Copy, [PB[hb]], [Bhsb[hsl_]])
                        OP("dve", "scalar_tensor_tensor", [Bhsb[hsl_], Bpar, Bd], [Bd], out=dst[:, 0:1], in0=hsb[hsl_][:, 0:1],
                           scalar=cp[:, 0:1], op0=ALU.mult, in1=dst[:, 0:1], op1=ALU.add)
                        OP("dve", "scalar_tensor_tensor", [Bhsb[hsl_], Bpar, Bd], [Bd], out=dst[:, NT - 1:NT], in0=hsb[hsl_][:, 1:2],
                           scalar=cp[:, 2:3], op0=ALU.mult, in1=dst[:, NT - 1:NT], op1=ALU.add)
                    ACT(sg[k2], cg[k2], AF.Silu, [Bcg[k2]], [Bsg[k2]])
                    OP("pool", "tensor_tensor", [Bsg[k2], Bcv[k2]], Bact, out=act[:, m, :], in0=sg[k2], in1=cv[k2], op=ALU.mult)
                if idx + 1 < len(tiles):
                    p3_prep_step(idx + 1, g)
            pend3 = []
            for sc in range(4):
                k = cnt3x[0] % 2
                cnt3x[0] += 1
                r0 = t0 + sc * 128
                DMA("sp", xr[k], XA[r0:r0 + 128, :], [BXA], [Bxr[k]], Bxr[k])
                yb = (6, 7) if sc % 2 == 0 else (1, 2)
                for half in range(2):
                    for m in range(22):
                        MM(pb[yb[half]], act[:, m, sc * 128:(sc + 1) * 128], wd[:, m, half * 512:(half + 1) * 512], m == 0, m == 21,
                           Bact + [BW0], [PB[yb[half]]])
                post_res_a(yb, k)
                if pend3:
                    post_res_b(*pend3.pop())
                pend3.append((yb, xr[k], Bxr[k], k, XOUT3, BXOUT3, r0))
            post_res_b(*pend3.pop())

        xr = [a32(D) for _ in range(2)]; Bxr = [Buf("xr%d" % i) for i in range(2)]
        cnt3x = [0]

        def post_res3(yb, k, r0):
            for half in range(2):
                ACT(junk[:, half * 512:(half + 1) * 512], pb[yb[half]], AF.Square, [PB[yb[half]]], [Bjunk, Bssh[k]],
                    accum_out=ssh[k][:, half:half + 1])
            OP("dve", "tensor_tensor", [Bssh[k]], [Bss2[k]], out=ss2[k], in0=ssh[k][:, 0:1], in1=ssh[k][:, 1:2], op=ALU.add)
            ACT(ry[k], ss2[k], AF.Sqrt, [Bss2[k]], [Bry[k]], scale=1.0 / D, bias=EPS)
            OP("dve", "reciprocal", [Bry[k]], [Bry[k]], out=ry[k], in_=ry[k])
            for half in range(2):
                OP("dve", "scalar_tensor_tensor", [PB[yb[half]], Bry[k], BG], [Btt[k]],
                   out=tt_[k][:, half * 512:(half + 1) * 512], in0=pb[yb[half]], scalar=ry[k], op0=ALU.mult,
                   in1=Gt[:, half * 512:(half + 1) * 512], op1=ALU.mult)
            OP("pool", "tensor_tensor", [Btt[k], Bxr[k]], [Btt[k]], out=tt_[k], in0=tt_[k], in1=xr[k], op=ALU.add)
            DMA("sp", XOUT3[r0:r0 + 128, :], tt_[k], [Btt[k]], [BXOUT3], Btt[k])

        p3_loadw(0)
        p3_loadw(1)
        p3_prep(0)
        for idx in range(len(tiles)):
            p3_main(idx)
        apos[0] = p1
        P.barrier(toks)

    P.emit(final_bufs=[BY])
    st.close()
    return nc


def make_consts():
    j = np.arange(128)[:, None]
    i = np.arange(128)[None, :]
    same = (j // 64) == (i // 64)
    triU = (same & (j <= i)).astype(np.float32)
    triL = (same & (j >= i)).astype(np.float32)
    c = np.zeros((128, 640), np.float32)
    c[:, 0:128] = np.eye(128, dtype=np.float32)
    c[:, 128:256] = triU * (-1.0 / 16)
    c[:, 256:384] = triL * (-1.0 / 16)
    c[:, 384:512] = triU
    c[:, 512:640] = triL
    return c


def pack_shared(w_ada, b_ada, g_pre_mix, g_post_mix, g_pre_ffn, g_post_ffn, w_in, w_s, b_s, g_vn, g_out_a,
                w_gf, b_gf, w_gb, b_gb, g_out_b, w_out, w_up, w_conv, b_conv, w_down):
    L = w_ada.shape[0]
    f = lambda a: np.ascontiguousarray(a, dtype=np.float32)
    fm8 = lambda v: v.reshape(L, 8, 128).transpose(0, 2, 1)
    fm4 = lambda v: v.reshape(L, 4, 128).transpose(0, 2, 1)
    d = {}
    d["w_ada"] = f(w_ada)
    d["badaF"] = f(b_ada.reshape(L, 48, 128).transpose(0, 2, 1))
    d["badaR"] = f(np.stack([b_ada[:, 2048:3072], b_ada[:, 5120:6144]], axis=1))
    d["gpreF"] = f(np.concatenate([fm8(g_pre_mix), fm8(g_pre_ffn)], axis=2))
    d["gpostR"] = f(np.stack([g_post_mix, g_post_ffn], axis=1))
    d["w_in"] = f(w_in)
    d["wsT"] = f(w_s.transpose(0, 3, 1, 2).reshape(L, 128, 512))
    d["bsR"] = f(b_s.reshape(L, 1, 512))
    d["ghF"] = f(np.concatenate([fm4(g_vn), fm4(g_out_a), fm4(g_out_b)], axis=2))
    d["w_g"] = f(np.stack([w_gf, w_gb], axis=1))
    d["bgR"] = f(np.concatenate([b_gf, b_gb], axis=1).reshape(L, 1, 512))
    d["w_out"] = f(w_out)
    d["w_up"] = f(w_up)
    d["w_down"] = f(w_down)
    cw = np.concatenate([w_conv, b_conv[:, None, :]], axis=1)
    d["convF"] = f(cw.reshape(L, 4, 44, 128).transpose(0, 3, 2, 1).reshape(L, 128, 176))
    d["consts"] = make_consts()
    return d


_NC_CACHE = {}


def kernel(x_prompt, x_sample, c_prompt, c_sample, w_ada, b_ada, g_pre_mix, g_post_mix,
           g_pre_ffn, g_post_ffn, w_in, w_s, b_s, g_vn, g_out_a, w_gf, b_gf, w_gb, b_gb,
           g_out_b, w_out, w_up, w_conv, b_conv, w_down):
    n = 8
    x_prompt = np.asarray(x_prompt, dtype=np.float32)
    x_sample = np.asarray(x_sample, dtype=np.float32)
    c_prompt = np.asarray(c_prompt, dtype=np.float32)
    c_sample = np.asarray(c_sample, dtype=np.float32)
    Bp, Tp, _ = x_prompt.shape
    Bs, Ts, _ = x_sample.shape
    ppc = Bp // n
    spc = Bs // n
    SEQS = [Tp] * ppc + [Ts] * spc
    L = int(np.asarray(w_ada).shape[0])
    shared = pack_shared(*[np.asarray(a, dtype=np.float32) for a in (
        w_ada, b_ada, g_pre_mix, g_post_mix, g_pre_ffn, g_post_ffn, w_in, w_s, b_s, g_vn, g_out_a,
        w_gf, b_gf, w_gb, b_gb, g_out_b, w_out, w_up, w_conv, b_conv, w_down)])
    key = (tuple(SEQS), L)
    if key not in _NC_CACHE:
        _NC_CACHE[key] = build_nc(SEQS, L)
    nc = _NC_CACHE[key]
    in_maps = []
    S = len(SEQS)
    for c in range(n):
        xs_ = [x_prompt[c * ppc + i] for i in range(ppc)] + [x_sample[c * spc + i] for i in range(spc)]
        cs_ = [c_prompt[c * ppc + i] for i in range(ppc)] + [c_sample[c * spc + i] for i in range(spc)]
        cc = np.stack(cs_, 0)
        cT = cc.reshape(S, 8, 128).transpose(2, 1, 0).reshape(128, 8 * S)
        m = dict(shared)
        m["x"] = np.ascontiguousarray(np.concatenate(xs_, 0))
        m["cT"] = np.ascontiguousarray(cT)
        in_maps.append(m)
    res = run_bass_kernel_spmd(nc, in_maps, core_ids=list(range(n)))
    yp = np.empty((Bp, Tp, D), np.float32)
    ys = np.empty((Bs, Ts, D), np.float32)
    for c in range(n):
        y = res.results[c]["y"]
        o = 0
        for i in range(ppc):
            yp[c * ppc + i] = y[o:o + Tp]
            o += Tp
        for i in range(spc):
            ys[c * spc + i] = y[o:o + Ts]
            o += Ts
    return (yp, ys)
```

```python
import os
import numpy as np
from contextlib import ExitStack
import concourse.bass as bass
import concourse.mybir as mybir
from concourse.bass_utils import run_bass_kernel_spmd

F32 = mybir.dt.float32
BF16 = mybir.dt.bfloat16
AF = mybir.ActivationFunctionType
ALU = mybir.AluOpType
AX = mybir.AxisListType

ENGS = ("pe", "act", "dve", "pool", "sp")
D = 1024
DFF = 2816
NT = 512
EPS = 1e-6


class _Buf:
    __slots__ = ("name", "wev", "rev", "dsem", "dcount")

    def __init__(self, name):
        self.name = name
        self.wev = {}
        self.rev = {}
        self.dsem = None
        self.dcount = 0


_BUFS = {}


def Buf(name):
    b = _BUFS.get(name)
    if b is None:
        b = _BUFS[name] = _Buf(name)
    return b


class Prog:
    def __init__(self, nc):
        self.nc = nc
        self.q = {e: [] for e in ENGS}
        self.dma_sems = {}
        self.dma_last = {}
        self.pending = {e: {} for e in ENGS}

    def _deps(self, eng, reads, writes, is_dma):
        waits = dict(self.pending[eng])
        self.pending[eng] = {}

        def need(key, idx):
            if waits.get(key, -1) < idx:
                waits[key] = idx

        for b in reads:
            for key, idx in b.wev.items():
                if key == eng and eng == "pe" and not is_dma:
                    continue
                need(key, idx)
        for b in writes:
            for key, idx in b.wev.items():
                if key == eng and eng == "pe" and not is_dma:
                    continue
                need(key, idx)
            for key, idx in b.rev.items():
                if key == eng and eng == "pe" and not is_dma:
                    continue
                need(key, idx)
        return waits

    def op(self, eng, fn, reads=(), writes=()):
        waits = self._deps(eng, reads, writes, False)
        idx = len(self.q[eng])
        self.q[eng].append(dict(fn=fn, waits=waits, dma=None, ms=False))
        for b in reads:
            b.rev[eng] = idx
        for b in writes:
            b.wev[eng] = idx
        return idx

    def dma(self, eng, fn, reads=(), writes=(), sem_buf=None):
        waits = self._deps(eng, reads, writes, True)
        kind = "sw" if eng == "pool" else "hw"
        if sem_buf.dsem is None:
            sem_buf.dsem = {}
            sem_buf.dcount = {}
        if kind not in sem_buf.dsem:
            sem_buf.dsem[kind] = ("dma", len(self.dma_sems))
            self.dma_sems[sem_buf.dsem[kind]] = None
            sem_buf.dcount[kind] = 0
        sem_buf.dcount[kind] += 16
        ev = (sem_buf.dsem[kind], sem_buf.dcount[kind])
        self.dma_last[ev[0]] = ev[1]
        self.q[eng].append(dict(fn=fn, waits=waits, dma=ev[0], ms=False))
        for b in reads:
            b.rev[ev[0]] = ev[1]
        for b in writes:
            b.wev[ev[0]] = ev[1]
        return ev

    def barrier(self, toks):
        for e in ("act", "dve", "pool"):
            fn, b = toks[e]
            self.op(e, fn, writes=[b])
        w = {}
        for e in ("act", "dve", "pool"):
            w[e] = len(self.q[e]) - 1
        for k, v in self.dma_last.items():
            w[k] = v
        for e in ENGS:
            p = self.pending[e]
            for k, v in w.items():
                if p.get(k, -1) < v:
                    p[k] = v

    def emit(self, final_bufs=()):
        nc = self.nc
        for e in ENGS:
            for o in self.q[e]:
                for key, idx in o["waits"].items():
                    if not isinstance(key, tuple):
                        self.q[key][idx]["ms"] = True
        ticks = {}
        for e in ENGS:
            t = 0
            for i, o in enumerate(self.q[e]):
                if o["ms"] and o["dma"] is None:
                    t += 1
                    ticks[(e, i)] = t
        with ExitStack() as st:
            esem = {e: st.enter_context(nc.semaphore("s_" + e)) for e in ENGS if e != "sp"}
            for k in list(self.dma_sems):
                self.dma_sems[k] = st.enter_context(nc.semaphore("d%d" % k[1]))
            block = st.enter_context(nc.Block())

            def run(e, eo):
                waited = {}

                def dowait(key, idx):
                    if isinstance(key, tuple):
                        sem, val = self.dma_sems[key], idx
                    else:
                        sem, val = esem[key], ticks[(key, idx)]
                    if waited.get(key, -1) >= val:
                        return
                    waited[key] = val
                    eo.wait_ge(sem, val)

                for o in self.q[e]:
                    for key, idx in o["waits"].items():
                        dowait(key, idx)
                    ins = o["fn"](eo)
                    if o["dma"] is not None:
                        ins.then_inc(self.dma_sems[o["dma"]], 16)
                    elif o["ms"]:
                        ins.then_inc(esem[e], 1)
                if e == "sp":
                    for b in final_bufs:
                        for key, idx in b.wev.items():
                            dowait(key, idx)

            @block.tensor
            def _(eo):
                run("pe", eo)

            @block.scalar
            def _(eo):
                run("act", eo)

            @block.vector
            def _(eo):
                run("dve", eo)

            @block.gpsimd
            def _(eo):
                run("pool", eo)

            @block.sync
            def _(eo):
                run("sp", eo)


def build_nc(SEQS, L):
    S = len(SEQS)
    TT = sum(SEQS)
    NSC = TT // 128
    NCH = TT // 64
    soff = [0]
    for n in SEQS:
        soff.append(soff[-1] + n)
    nc = bass.Bass("TRN2", target_bir_lowering=False)
    _BUFS.clear()

    def din(name, shape, dt=F32):
        return nc.dram_tensor(name, shape, dt, kind="ExternalInput").ap()

    def dscr(name, shape, dt):
        return nc.dram_tensor(name, shape, dt, kind="Internal").ap()

    X = din("x", [TT, D])
    cT = din("cT", [128, 8 * S])
    w_ada = din("w_ada", [L, D, 6 * D])
    badaF = din("badaF", [L, 128, 48])
    badaR = din("badaR", [L, 2, D])
    gpreF = din("gpreF", [L, 128, 16])
    gpostR = din("gpostR", [L, 2, D])
    w_in = din("w_in", [L, D, 2592])
    wsT = din("wsT", [L, 128, 512])
    bsR = din("bsR", [L, 1, 512])
    ghF = din("ghF", [L, 128, 12])
    w_g = din("w_g", [L, 2, 16, 256])
    bgR = din("bgR", [L, 1, 512])
    w_out = din("w_out", [L, D, D])
    w_up = din("w_up", [L, D, 2 * DFF])
    w_down = din("w_down", [L, DFF, D])
    convF = din("convF", [L, 128, 176])
    consts = din("consts", [128, 640])
    Y = nc.dram_tensor("y", [TT, D], F32, kind="ExternalOutput").ap()

    XA = dscr("xa", [TT, D], F32)
    XB = dscr("xb", [TT, D], F32)
    A_s = dscr("a_s", [NSC, 128, 512], BF16)
    O_s = dscr("o_s", [NSC, 128, 512], F32)
    SR_s = dscr("sr_s", [NSC, 128, 512], BF16)
    V_s = dscr("v_s", [NSC, 128, 512], BF16)
    KD_s = dscr("kd_s", [NSC, 128, 256], BF16)
    QD_s = dscr("qd_s", [NSC, 128, 256], BF16)
    G_s = dscr("g_s", [L, 2, S, D], F32)
    WUP_s = dscr("wup_s", [L, 11, 128, 4096], BF16)

    st = ExitStack()
    ARW = 52000
    ar = st.enter_context(nc.sbuf_tensor("arena", [128, ARW], F32))
    pbt = [st.enter_context(nc.psum_tensor("pb%d" % i, [128, 512], F32)) for i in range(8)]
    PB = [Buf("pb%d" % i) for i in range(8)]
    pb = [t[:] for t in pbt]
    pbh = [t[:].bitcast(BF16) for t in pbt]

    P = Prog(nc)
    apos = [0]

    def a32(n):
        o = apos[0]
        apos[0] += n
        assert apos[0] <= ARW, ("sbuf arena overflow", apos[0])
        return ar[:, o:o + n]

    def a16(n):
        assert n % 2 == 0
        return a32(n // 2).bitcast(BF16)

    def OP(eng, method, reads, writes, *args, **kw):
        P.op(eng, lambda e: getattr(e, method)(*args, **kw), reads, writes)

    def DMA(eng, out, in_, reads, writes, sem_buf):
        P.dma(eng, lambda e: e.dma_start(out=out, in_=in_), reads, writes, sem_buf)

    def MM(out, lhsT, rhs, start, stop, reads, writes):
        P.op("pe", lambda e: e.matmul(out, lhsT=lhsT, rhs=rhs, start=start, stop=stop), reads, writes)

    def TR(out, in_, ident, reads, writes):
        P.op("pe", lambda e: e.transpose(out=out, in_=in_, identity=ident), reads, writes)

    def ACT(out, in_, func, reads, writes, **kw):
        P.op("act", lambda e: e.activation(out=out, in_=in_, func=func, **kw), reads, writes)

    cst = a32(640); Bcst = Buf("cst")
    identb = a16(128); maskU = a16(128); maskL = a16(128); ones128 = a16(128); Bcb = Buf("cb")
    triU = cst[:, 128:256]; triL = cst[:, 256:384]
    gpre = a32(L * 16); gh = a32(L * 12); cvp = a32(L * 176); badaf = a32(L * 48); Bpar = Buf("par")
    bsB = a32(L * 512); bgB = a32(L * 512)
    wsTb = a16(L * 512); Wg = a16(L * 512)
    scb = a16(8 * S); cTs = a32(8 * S); Bsc = Buf("sc")
    modA1 = [a32(8 * S) for _ in range(L)]; modB1 = [a32(8 * S) for _ in range(L)]
    modA2 = [a32(8 * S) for _ in range(L)]; modB2 = [a32(8 * S) for _ in range(L)]
    Bmod = Buf("mod")
    DECF = a32(NCH * 2); DECB = a32(NCH * 2); Bdecf = Buf("decf"); Bdecb = Buf("decb")
    neghalf = a32(1); Bnh = Buf("neghalf")
    tokt = a32(4); Btok = {e: Buf("tok" + e) for e in ("act", "dve", "pool")}
    xs = [a32(D) for _ in range(2)]; Bx = [Buf("x%d" % i) for i in range(2)]
    hn = [a16(D) for _ in range(2)]; Bhn = [Buf("hn%d" % i) for i in range(2)]
    junk = a16(D); Bjunk = Buf("junk")
    ssq = [a32(1) for _ in range(2)]; rsd = [a32(1) for _ in range(2)]
    Bss = [Buf("ss%d" % i) for i in range(2)]; Brs = [Buf("rs%d" % i) for i in range(2)]
    hT = [a16(8 * NT).rearrange("p (k t) -> p k t", k=8) for _ in range(2)]
    BhT = [[Buf("hT%d_%d" % (i, j)) for j in range(4)] for i in range(2)]
    hTh = [a16(16).rearrange("p (k t) -> p k t", k=8) for _ in range(2)]
    BhTh = [Buf("hTh%d" % i) for i in range(2)]
    W0 = a16(22 * 1024); BW0 = Buf("W0")
    W1 = a16(8 * 1024); BW1 = Buf("W1")
    wi = W0[:, 0:8 * 2624].rearrange("p (k c) -> p k c", k=8)
    wd = W0.rearrange("p (k c) -> p k c", k=22)
    wo = W1.rearrange("p (k c) -> p k c", k=8)
    Gt = a32(D); BG = Buf("G")
    tt_ = [a32(D) for _ in range(2)]; Btt = [Buf("t%d" % i) for i in range(2)]
    ss2 = [a32(1) for _ in range(2)]; ry = [a32(1) for _ in range(2)]
    Bss2 = [Buf("ss2%d" % i) for i in range(2)]; Bry = [Buf("ry%d" % i) for i in range(2)]
    PBASE = apos[0]

    toks = {
        "act": (lambda e: e.activation(out=tokt[0:1, 0:1], in_=tokt[0:1, 0:1], func=AF.Copy), Btok["act"]),
        "dve": (lambda e: e.memset(tokt[0:1, 1:2], 0.0), Btok["dve"]),
        "pool": (lambda e: e.memset(tokt[0:1, 2:3], 0.0), Btok["pool"]),
    }
    OP("dve", "memset", [], [Btok["act"]], tokt[0:1, 0:1], 0.0)
    OP("pool", "memset", [], [Bnh], neghalf, -0.5)

    def RSQ(out, in_, scale, rd, wr, np_=128, nf=1):
        OP("pool", "tensor_scalar", rd, wr, out=out, in0=in_, scalar1=scale, scalar2=EPS, op0=ALU.mult, op1=ALU.add)
        OP("pool", "tensor_tensor", wr + [Bnh], wr, out=out, in0=out, in1=neghalf[0:np_, 0:1].to_broadcast([np_, nf]), op=ALU.pow)

    BXA = Buf("XA"); BXB = Buf("XB"); BY = Buf("Y")
    Bscr = {n: Buf(n) for n in ("A", "O", "SR", "V", "KD", "QD")}
    BGs = Buf("Gs"); BWUP = Buf("WUP")

    DMA("sp", cst, consts, [], [Bcst], Bcst)
    OP("dve", "tensor_copy", [Bcst], [Bcb], out=identb, in_=cst[:, 0:128])
    OP("dve", "tensor_copy", [Bcst], [Bcb], out=maskU, in_=cst[:, 384:512])
    OP("dve", "tensor_copy", [Bcst], [Bcb], out=maskL, in_=cst[:, 512:640])
    OP("dve", "memset", [], [Bcb], ones128, 1.0 / 128)
    OP("pool", "memset", [], [Bpar], Wg, 0.0)
    for l in range(L):
        DMA("sp", gpre[:, l * 16:(l + 1) * 16], gpreF[l], [], [Bpar], Bpar)
        DMA("sp", gh[:, l * 12:(l + 1) * 12], ghF[l], [], [Bpar], Bpar)
        DMA("sp", cvp[:, l * 176:(l + 1) * 176], convF[l], [], [Bpar], Bpar)
        DMA("sp", badaf[:, l * 48:(l + 1) * 48], badaF[l], [], [Bpar], Bpar)
        DMA("sp", bsB[:, l * 512:(l + 1) * 512], bsR[l].partition_broadcast(128), [], [Bpar], Bpar)
        DMA("sp", bgB[:, l * 512:(l + 1) * 512], bgR[l].partition_broadcast(128), [], [Bpar], Bpar)
        DMA("pool", wsTb[:, l * 512:(l + 1) * 512], wsT[l], [], [Bpar], Bpar)
        DMA("pool", Wg[0:16, l * 512:l * 512 + 256], w_g[l, 0], [], [Bpar], Bpar)
        DMA("pool", Wg[32:48, l * 512 + 256:(l + 1) * 512], w_g[l, 1], [], [Bpar], Bpar)
    DMA("sp", cTs, cT, [], [Bsc], Bsc)
    ACT(scb, cTs, AF.Silu, [Bsc], [Bsc])

    pm = apos[0]
    wad = [a16(8 * 1024).rearrange("p (k c) -> p k c", k=8) for _ in range(2)]
    Bwad = [Buf("wad%d" % i) for i in range(2)]
    brow = a32(D); grow = a32(D); prow = a32(D); Brow = Buf("row"); Bgrow = [Buf("grow")]
    mraw = a32(8 * S); Bmraw = Buf("mraw")
    wstg = [a16(4096) for _ in range(2)]; Bwstg = [Buf("wstg%d" % i) for i in range(2)]
    wi_n = 0
    for l in range(L):
        wv = w_ada[l].rearrange("(k p) c -> p k c", p=128)
        for j in range(6):
            sl = wi_n % 2
            wi_n += 1
            for k0 in range(0, 8, 4):
                DMA("pool", wad[sl][:, k0:k0 + 4, :], wv[:, k0:k0 + 4, j * D:(j + 1) * D], [], [Bwad[sl]], Bwad[sl])
            if j in (0, 1, 3, 4):
                pv = pb[0][:, 0:8 * S].rearrange("p (c s) -> p c s", c=8)
                for c8 in range(8):
                    for kc in range(8):
                        MM(pv[:, c8, :], wad[sl][:, kc, c8 * 128:(c8 + 1) * 128], scb[:, kc * S:(kc + 1) * S],
                           kc == 0, kc == 7, [Bwad[sl], Bsc], [PB[0]])
                bv = badaf[:, l * 48 + j * 8: l * 48 + (j + 1) * 8].unsqueeze(2).to_broadcast([128, 8, S])
                tgt = {0: modB1, 1: modA1, 3: modB2, 4: modA2}[j][l]
                tv = tgt.rearrange("p (c s) -> p c s", c=8)
                if j in (0, 3):
                    OP("dve", "tensor_tensor", [PB[0], Bpar], [Bmod], out=tv, in0=pv, in1=bv, op=ALU.add)
                else:
                    mv = mraw.rearrange("p (c s) -> p c s", c=8)
                    OP("dve", "tensor_tensor", [PB[0], Bpar], [Bmraw], out=mv, in0=pv, in1=bv, op=ALU.add)
                    go = 0 if j == 1 else 8
                    gv = gpre[:, l * 16 + go: l * 16 + go + 8].unsqueeze(2).to_broadcast([128, 8, S])
                    OP("dve", "scalar_tensor_tensor", [Bmraw, Bpar], [Bmod], out=tv, in0=mv, scalar=1.0,
                       op0=ALU.add, in1=gv, op1=ALU.mult)
            else:
                which = 0 if j == 2 else 1
                DMA("sp", brow[0:S, :], badaR[l, which:which + 1, :].partition_broadcast(S), [], [Brow], Brow)
                DMA("sp", prow[0:S, :], gpostR[l, which:which + 1, :].partition_broadcast(S), [], [Brow], Brow)
                for half in range(2):
                    for kc in range(8):
                        MM(pb[1][0:S, :], scb[:, kc * S:(kc + 1) * S], wad[sl][:, kc, half * 512:(half + 1) * 512],
                           kc == 0, kc == 7, [Bwad[sl], Bsc], [PB[1]])
                    OP("dve", "tensor_tensor", [PB[1], Brow], Bgrow, out=grow[0:S, half * 512:(half + 1) * 512],
                       in0=pb[1][0:S, :], in1=brow[0:S, half * 512:(half + 1) * 512], op=ALU.add)
                OP("dve", "tensor_tensor", Bgrow + [Brow], Bgrow, out=grow[0:S, :], in0=grow[0:S, :], in1=prow[0:S, :], op=ALU.mult)
                DMA("sp", G_s[l, which], grow[0:S, :], Bgrow, [BGs], Bgrow[0])
        uv = w_up[l].rearrange("(k p) c -> p k c", p=128)
        for g in range(11):
            sl = g % 2
            sv = wstg[sl].rearrange("p (k a c) -> p k a c", k=8, a=2)
            DMA("pool", sv[:, :, 0, :], uv[:, :, g * 256:(g + 1) * 256], [], [Bwstg[sl]], Bwstg[sl])
            DMA("pool", sv[:, :, 1, :], uv[:, :, DFF + g * 256: DFF + (g + 1) * 256], [], [Bwstg[sl]], Bwstg[sl])
            DMA("sp", WUP_s[l, g], wstg[sl], [Bwstg[sl]], [BWUP], Bwstg[sl])
    apos[0] = pm
    P.barrier(toks)
    STOP = os.environ.get("MK_STOP", "")

    class _Stop(Exception):
        pass

    def CK(name):
        if STOP == name:
            raise _Stop()
    if STOP == "pro":
        P.emit(final_bufs=[BY])
        st.close()
        return nc

    def load_x(src, Bsrc, r0, sl):
        DMA("sp", xs[sl], src[r0:r0 + 128, :], [Bsrc], [Bx[sl]], Bx[sl])

    def norm_part1(sl, np_=128):
        ACT(junk[0:np_, :], xs[sl][0:np_, :], AF.Square, [Bx[sl]], [Bjunk, Bss[sl]], accum_out=ssq[sl][0:np_, :])
        RSQ(rsd[sl][0:np_, :], ssq[sl][0:np_, :], 1.0 / D, [Bss[sl]], [Brs[sl]], np_=np_)

    def norm_part2(sl, np_=128):
        ACT(hn[sl][0:np_, :], xs[sl][0:np_, :], AF.Copy, [Bx[sl], Brs[sl]], [Bhn[sl]], scale=rsd[sl][0:np_, :])

    def norm_to_hn(sl, np_=128):
        norm_part1(sl, np_)
        norm_part2(sl, np_)

    def prep_sub(src, Bsrc, t0, s, hsl, mA, mB, sc, sl, part):
        if part == 0:
            load_x(src, Bsrc, t0 + sc * 128, sl)
            norm_part1(sl)
        elif part == 1:
            norm_part2(sl)
        else:
            psT = pbh[0].rearrange("p (k t) -> p k t", k=8)
            for kc in range(8):
                TR(psT[:, kc, :], hn[sl][:, kc * 128:(kc + 1) * 128], identb, [Bhn[sl], Bcb], [PB[0]])
            for kc in range(8):
                ACT(hT[hsl][:, kc, sc * 128:(sc + 1) * 128], psT[:, kc, :], AF.Identity, [PB[0], Bmod], [BhT[hsl][sc]],
                    scale=mA[:, kc * S + s: kc * S + s + 1], bias=mB[:, kc * S + s: kc * S + s + 1])

    def prep_tile(src, Bsrc, t0, s, hsl, mA, mB, cnt):
        psT = pbh[0].rearrange("p (k t) -> p k t", k=8)
        for sc in range(4):
            sl = cnt[0] % 2
            cnt[0] += 1
            load_x(src, Bsrc, t0 + sc * 128, sl)
            norm_to_hn(sl)
            for kc in range(8):
                TR(psT[:, kc, :], hn[sl][:, kc * 128:(kc + 1) * 128], identb, [Bhn[sl], Bcb], [PB[0]])
            for kc in range(8):
                OP("dve", "tensor_scalar", [PB[0], Bmod], [BhT[hsl][sc]], out=hT[hsl][:, kc, sc * 128:(sc + 1) * 128],
                   in0=psT[:, kc, :], scalar1=mA[:, kc * S + s: kc * S + s + 1], scalar2=mB[:, kc * S + s: kc * S + s + 1],
                   op0=ALU.mult, op1=ALU.add)

    ssh = [a32(2) for _ in range(2)]; Bssh = [Buf("ssh%d" % i) for i in range(2)]

    def post_res_a(yb, k2):
        for half in range(2):
            ACT(junk[:, half * 512:(half + 1) * 512], pb[yb[half]], AF.Square, [PB[yb[half]]], [Bjunk, Bssh[k2]],
                accum_out=ssh[k2][:, half:half + 1])
        OP("pool", "tensor_tensor", [Bssh[k2]], [Bss2[k2]], out=ss2[k2], in0=ssh[k2][:, 0:1], in1=ssh[k2][:, 1:2], op=ALU.add)
        RSQ(ry[k2], ss2[k2], 1.0 / D, [Bss2[k2]], [Bry[k2]])

    def post_res_b(yb, xap, Bxb, k2, dst, Bdst, r0):
        for half in range(2):
            OP("dve", "scalar_tensor_tensor", [PB[yb[half]], Bry[k2], BG], [Btt[k2]],
               out=tt_[k2][:, half * 512:(half + 1) * 512], in0=pb[yb[half]], scalar=ry[k2], op0=ALU.mult,
               in1=Gt[:, half * 512:(half + 1) * 512], op1=ALU.mult)
        OP("pool", "tensor_tensor", [Btt[k2], Bxb], [Btt[k2]], out=tt_[k2], in0=tt_[k2], in1=xap, op=ALU.add)
        DMA("pool", dst[r0:r0 + 128, :], tt_[k2], [Btt[k2]], [Bdst], Btt[k2])

    def post_res(yb, xap, Bxb, k2, dst, Bdst, r0):
        post_res_a(yb, k2)
        post_res_b(yb, xap, Bxb, k2, dst, Bdst, r0)

    def load_G(l, which, s):
        DMA("sp", Gt, G_s[l, which, s:s + 1, :].partition_broadcast(128), [BGs], [BG], BG)

    tiles = []
    for s in range(S):
        for ti in range(SEQS[s] // NT):
            tiles.append((s, ti, soff[s] + ti * NT))

    for l in range(L):
        XIN, BXIN = (X, Buf("Xin")) if l == 0 else (XB, BXB)
        XOUT3, BXOUT3 = (Y, BY) if l == L - 1 else (XB, BXB)
        lp = l * 512
        p1 = apos[0]
        wv = w_in[l].rearrange("(k p) c -> p k c", p=128)
        OP("pool", "memset", [], [BW0], wi[:, :, 1536:1600], 0.0)
        for (d0, s0, n) in ((0, 0, 512), (512, 1024, 256), (768, 1280, 256), (1024, 2048, 512),
                            (1536, 2560, 16), (1568, 2576, 16), (1600, 512, 512), (2112, 1536, 512)):
            DMA("pool", wi[:, :, d0:d0 + n], wv[:, :, s0:s0 + n], [], [BW0], BW0)
        ov = w_out[l].rearrange("(k p) c -> p k c", p=128)
        for k0 in range(0, 8, 4):
            DMA("pool", wo[:, k0:k0 + 4, :], ov[:, k0:k0 + 4, :], [], [BW1], BW1)

        uTs = [a32(4 * NT).rearrange("p (k t) -> p k t", k=4) for _ in range(2)]; BuTs = [Buf("uT0"), Buf("uT1")]
        qT = a32(2 * NT).rearrange("p (k t) -> p k t", k=2); kT = a32(2 * NT).rearrange("p (k t) -> p k t", k=2)
        BqT = Buf("qT"); BkT = Buf("kT")
        srt = a16(4 * NT).rearrange("p (k t) -> p k t", k=4); Bsr = Buf("sr")
        lT = a16(NT); BlT = Buf("lT")
        vbf = [a16(512) for _ in range(3)]; Bvbf = [Buf("vbf%d" % i) for i in range(3)]
        zt = a32(512); Bzt = Buf("zt")
        spt = a32(512); Bsp = Buf("sp")
        Ep = a32(512).rearrange("p (a t) -> p a t", a=4); Em = a32(512).rearrange("p (a t) -> p a t", a=4)
        BEp = Buf("Ep"); BEm = Buf("Em")
        qd = [a16(512).rearrange("p (a t) -> p a t", a=4) for _ in range(3)]; Bqd = [Buf("qd%d" % i) for i in range(3)]
        kds = [a16(512).rearrange("p (a t) -> p a t", a=4) for _ in range(2)]; Bkds = [Buf("kd%d" % i) for i in range(2)]
        kdt = [a16(512).rearrange("p (a t) -> p a t", a=4) for _ in range(3)]; Bkdt = [Buf("kdt%d" % i) for i in range(3)]
        scms = [[a16(512).rearrange("p (a t) -> p a t", a=4) for _ in range(2)] for _ in range(2)]
        Bscms = [[Buf("scm%d_%d" % (i, j)) for j in range(2)] for i in range(2)]
        Pst = a32(512); BPst = Buf("Pst")
        Sbf = [a16(512) for _ in range(2)]; BSbf = [Buf("Sbf0"), Buf("Sbf1")]
        vsq = a32(512); Bvsq = Buf("vsq")
        vsb = a32(512); Bvsb = Buf("vsb")
        st1 = a32(4); st2 = a32(4); mean = a32(4); msq = a32(4); var = a32(4); rsv = a32(4)
        Bst = Buf("st")
        vhats = [a16(512) for _ in range(2)]; Bvhats = [Buf("vhat0"), Buf("vhat1")]
        tmx = a32(512); Btmx = Buf("tmx")
        apre = a32(512); Bapre = Buf("apre")
        asq = a16(512); Basq = Buf("asq")
        ra = a32(512); Bra = Buf("ra")
        aTt = [a16(512) for _ in range(2)]; BaT = [Buf("aT%d" % i) for i in range(2)]
        oloc = [a32(512) for _ in range(2)]; Boloc = [Buf("oloc%d" % i) for i in range(2)]

        cnt = [0]
        hsl_n = [0]

        def p1_prep(idx):
            s, ti, t0 = tiles[idx]
            prep_tile(XIN, BXIN, t0, s, idx % 2, modA1[l], modB1[l], cnt)

        def p1_phaseA(idx):
            hsl = idx % 2
            h = hT[hsl]
            rd = BhT[hsl] + [BW0]
            chunks = [("u", i, i * 128, 128) for i in range(4)] + [("q", i, 512 + i * 128, 128) for i in range(2)] + \
                     [("k", i, 768 + i * 128, 128) for i in range(2)] + [("r", i, 1024 + i * 128, 128) for i in range(4)] + \
                     [("l", 0, 1536, 64)]
            for ci, (kind, i, c0, m) in enumerate(chunks):
                b = 1 + ci % 2
                for kc in range(8):
                    MM(pb[b][0:m, :], wi[:, kc, c0:c0 + m], h[:, kc, :], kc == 0, kc == 7, rd, [PB[b]])
                if kind == "u":
                    ACT(uTs[hsl][:, i, :], pb[b], AF.Copy, [PB[b]], [BuTs[hsl]])
                elif kind == "q":
                    ACT(qT[:, i, :], pb[b], AF.Copy, [PB[b]], [BqT], scale=0.125)
                elif kind == "k":
                    ACT(kT[:, i, :], pb[b], AF.Copy, [PB[b]], [BkT])
                elif kind == "r":
                    ACT(srt[:, i, :], pb[b], AF.Silu, [PB[b]], [Bsr])
                else:
                    OP("dve", "tensor_copy", [PB[b]], [BlT], out=lT[0:64, :], in_=pb[b][0:64, :])

        subs = []
        for idx in range(len(tiles)):
            s_, ti_, t0_ = tiles[idx]
            for sc in range(4):
                subs.append((idx, sc, ti_ == 0 and sc == 0))
        NSUB = len(subs)

        def p1_A(n):
            idx, sc, seq_first = subs[n]
            s, ti, t0 = tiles[idx]
            hsl = idx % 2
            h = hT[hsl]
            gsc = (t0 + sc * 128) // 128
            k3 = n % 3
            k2 = n % 2
            cs = slice(sc * 128, (sc + 1) * 128)
            kd = kds[k2]; Bkd = Bkds[k2]
            vhat = vhats[k2]; Bvhat = Bvhats[k2]
            for kc in range(8):
                MM(pb[4], h[:, kc, cs], wi[:, kc, 2112:2624], kc == 0, kc == 7, [BhT[hsl][sc], BW0], [PB[4]])
            MM(pb[5], lT[0:48, cs], Wg[0:48, lp:lp + 512], True, True, [BlT, Bpar], [PB[5]])
            for kc in range(8):
                MM(pb[3], h[:, kc, cs], wi[:, kc, 1600:2112], kc == 0, kc == 7, [BhT[hsl][sc], BW0], [PB[3]])
            ACT(vbf[k3], pb[4], AF.Copy, [PB[4]], [Bvbf[k3]])
            DMA("sp", V_s[gsc], vbf[k3], [Bvbf[k3]], [Bscr["V"]], Bvbf[k3])
            OP("dve", "tensor_tensor", [PB[5], Bpar], [Bzt], out=zt, in0=pb[5], in1=bgB[:, lp:lp + 512], op=ALU.add)
            ACT(spt, zt, AF.Exp, [Bzt], [Bsp], scale=-1.0)
            ACT(spt, spt, AF.Ln, [Bsp], [Bsp], bias=1.0)
            OMIT = os.environ.get("MK_OMIT", "")
            ACT(vsb, pb[3], AF.Copy, [PB[3]], [Bvsb])
            v3 = vsb.rearrange("p (h t) -> p h t", h=4)
            if "s" not in OMIT:
                OP("dve", "tensor_reduce", [Bvsb], [Bst], out=st1, in_=v3, axis=AX.X, op=ALU.add)
                ACT(vsq, vsb, AF.Square, [Bvsb], [Bvsq])
                OP("dve", "tensor_reduce", [Bvsq], [Bst], out=st2, in_=vsq.rearrange("p (h t) -> p h t", h=4), axis=AX.X, op=ALU.add)
            if "s" not in OMIT:
                OP("dve", "tensor_scalar", [Bst], [Bst], out=mean, in0=st1, scalar1=1.0 / 128, scalar2=None, op0=ALU.mult)
                OP("dve", "tensor_tensor", [Bst], [Bst], out=msq, in0=mean, in1=mean, op=ALU.mult)
                OP("dve", "scalar_tensor_tensor", [Bst], [Bst], out=var, in0=st2, scalar=1.0 / 128, op0=ALU.mult, in1=msq, op1=ALU.subtract)
                RSQ(rsv, var, 1.0, [Bst], [Bst], nf=4)
            yield
            bT = pb[6].rearrange("p (a t) -> p a t", a=4)
            for dr in range(2):
                for dc in range(2):
                    a = dr * 2 + dc
                    if "f" in OMIT:
                        continue
                    MM(bT[:, a, :], spt[:, dr * 256 + dc * 128: dr * 256 + (dc + 1) * 128], triU if dr == 0 else triL,
                       True, True, [Bsp, Bcst], [PB[6]])
            ACT(Ep, bT, AF.Exp, [PB[6]], [BEp])
            ACT(Em, bT, AF.Exp, [PB[6]], [BEm], scale=-1.0)
            for hh in range(4 if "s" not in OMIT else 0):
                OP("dve", "tensor_scalar", [Bvsb, Bst], [Bvhat], out=vhat[:, hh * 128:(hh + 1) * 128], in0=v3[:, hh, :],
                   scalar1=mean[:, hh:hh + 1], scalar2=rsv[:, hh:hh + 1], op0=ALU.subtract, op1=ALU.mult)
            for c in range(2):
                gc = gsc * 2 + c
                OP("pool", "tensor_copy", [BEp], [Bdecf], out=DECF[:, gc * 2:gc * 2 + 2], in_=Ep[:, 0:2, c * 64 + 63])
                OP("pool", "tensor_copy", [BEp], [Bdecb], out=DECB[:, gc * 2:gc * 2 + 2], in_=Ep[:, 2:4, c * 64])
            qv = qT[:, :, cs].unsqueeze(1).to_broadcast([128, 2, 2, 128])
            kv = kT[:, :, cs].unsqueeze(1).to_broadcast([128, 2, 2, 128])
            OP("pool", "tensor_tensor", [BkT, BEm], [Bkd], out=kd.rearrange("p (r c) t -> p r c t", r=2),
               in0=kv, in1=Em.rearrange("p (r c) t -> p r c t", r=2), op=ALU.mult)
            OP("pool", "tensor_tensor", [BqT, BEp], [Bqd[k3]], out=qd[k3].rearrange("p (r c) t -> p r c t", r=2),
               in0=qv, in1=Ep.rearrange("p (r c) t -> p r c t", r=2), op=ALU.mult)
            DMA("sp", QD_s[gsc].rearrange("p (a t) -> p a t", a=2), qd[k3][:, 2:4, :], [Bqd[k3]], [Bscr["QD"]], Bqd[k3])
            yield
            kdT = pbh[7][:, 0:512].rearrange("p (a t) -> p a t", a=4)
            if "k" not in OMIT:
                for a in range(4):
                    TR(kdT[:, a, :], kd[:, a, :], identb, [Bkd, Bcb], [PB[7]])
                ACT(kdt[k3], kdT, AF.Copy, [PB[7]], [Bkdt[k3]])
                DMA("sp", KD_s[gsc].rearrange("p (a t) -> p a t", a=2), kdt[k3][:, 2:4, :], [Bkdt[k3]], [Bscr["KD"]], Bkdt[k3])
            DMA("sp", SR_s[gsc].rearrange("p (h t) -> p h t", h=4), srt[:, :, cs], [Bsr], [Bscr["SR"]], Bsr)

        def p1_B(n):
            idx, sc, seq_first = subs[n]
            s, ti, t0 = tiles[idx]
            hsl = idx % 2
            gsc = (t0 + sc * 128) // 128
            k3 = n % 3
            k2 = n % 2
            cs = slice(sc * 128, (sc + 1) * 128)
            kd = kds[k2]; Bkd = Bkds[k2]
            vhat = vhats[k2]; Bvhat = Bvhats[k2]
            scm = scms[k2]; Bscm = Bscms[k2]
            uT = uTs[hsl]; BuT = BuTs[hsl]
            mx = pb[3].rearrange("p (h t) -> p h t", h=4)
            for hh in range(4):
                MM(mx[:, hh, :], vhat[:, hh * 128:(hh + 1) * 128], wsTb[:, lp + hh * 128: lp + (hh + 1) * 128], True, True,
                   [Bvhat, Bpar], [PB[3]])
            for dr in range(2):
                for hh in range(4):
                    a = dr * 2 + hh // 2
                    ps_ = slice((hh % 2) * 64, (hh % 2) * 64 + 64)
                    sb_ = 1 + hh % 2
                    sv = pb[sb_].rearrange("p (h t) -> p h t", h=4)
                    MM(sv[:, dr * 2 + hh // 2, :], kd[ps_, a, :], qd[k3][ps_, a, :], True, True, [Bkd, Bqd[k3]], [PB[sb_]])
            for hh in range(4):
                OP("dve", "scalar_tensor_tensor", [PB[3], Bpar], [Btmx], out=tmx[:, hh * 128:(hh + 1) * 128], in0=mx[:, hh, :],
                   scalar=gh[:, l * 12 + hh: l * 12 + hh + 1], op0=ALU.mult, in1=bsB[:, lp + hh * 128: lp + (hh + 1) * 128], op1=ALU.add)
            OP("pool", "tensor_tensor", [Btmx, BuT], [Bapre], out=apre.rearrange("p (h t) -> p h t", h=4),
               in0=tmx.rearrange("p (h t) -> p h t", h=4), in1=uT[:, :, cs], op=ALU.mult)
            for par in range(2):
                sv = pb[1 + par].rearrange("p (h t) -> p h t", h=4)
                for dr in range(2):
                    OP("dve", "tensor_tensor", [PB[1 + par], Bcb], [Bscm[dr]],
                       out=scm[dr].rearrange("p (g e) t -> p g e t", e=2)[:, :, par, :], in0=sv[:, dr * 2:dr * 2 + 2, :],
                       in1=(maskU if dr == 0 else maskL).unsqueeze(1).to_broadcast([128, 2, 128]), op=ALU.mult)
            yield
            ACT(asq, apre, AF.Square, [Bapre], [Basq])
            yield
            MM(pb[7], ones128, asq, True, True, [Basq, Bcb], [PB[7]])
            ACT(ra, pb[7], AF.Ln, [PB[7]], [Bra], bias=EPS)
            ACT(ra, ra, AF.Exp, [Bra], [Bra], scale=-0.5)
            for hh in range(4):
                OP("dve", "scalar_tensor_tensor", [Bapre, Bpar, Bra], [BaT[k2]], out=aTt[k2][:, hh * 128:(hh + 1) * 128],
                   in0=apre[:, hh * 128:(hh + 1) * 128], scalar=gh[:, l * 12 + 4 + hh: l * 12 + 5 + hh], op0=ALU.mult,
                   in1=ra[:, hh * 128:(hh + 1) * 128], op1=ALU.mult)
            DMA("sp", A_s[gsc], aTt[k2], [BaT[k2]], [Bscr["A"]], BaT[k2])

        def p1_C(n):
            idx, sc, seq_first = subs[n]
            s, ti, t0 = tiles[idx]
            gsc = (t0 + sc * 128) // 128
            k3 = n % 3
            k2 = n % 2
            scm = scms[k2]; Bscm = Bscms[k2]
            if seq_first:
                OP("dve", "memset", [], [BPst], Pst, 0.0)
                OP("dve", "memset", [], [BSbf[0]], Sbf[0], 0.0)
                OP("dve", "memset", [], [BSbf[1]], Sbf[1], 0.0)
            ovb = (pb[4][:, 0:256].rearrange("p (h t) -> p h t", h=2), pb[3][:, 0:256].rearrange("p (h t) -> p h t", h=2))
            obk = (4, 3)
            dsb = (5, 6)
            for c in range(2):
                db = dsb[c]
                for hp in range(2):
                    MM(pb[db][:, hp * 256:(hp + 1) * 256], kdt[k3][c * 64:(c + 1) * 64, hp, :],
                       vbf[k3][c * 64:(c + 1) * 64, hp * 256:(hp + 1) * 256], True, True, [Bkdt[k3], Bvbf[k3]], [PB[db]])
            for c in range(2):
                gc = gsc * 2 + c
                first = seq_first and c == 0
                if c == 0:
                    for hh in range(4):
                        ovh = ovb[hh % 2][:, hh // 2, :]
                        MM(ovh, vbf[k3][:, hh * 128:(hh + 1) * 128], scm[0][:, hh, :], hh < 2, False,
                           [Bvbf[k3], Bscm[0]], [PB[obk[hh % 2]]])
                        MM(ovh, vbf[k3][:, hh * 128:(hh + 1) * 128], scm[1][:, hh, :], False, False,
                           [Bvbf[k3], Bscm[1]], [PB[obk[hh % 2]]])
                sidx = gc % 2
                for hh in range(4):
                    ps_ = slice((hh % 2) * 64, (hh % 2) * 64 + 64)
                    col = (hh // 2) * 256 + (hh % 2) * 128
                    MM(ovb[hh % 2][:, hh // 2, c * 64:(c + 1) * 64], Sbf[sidx][ps_, col:col + 128],
                       qd[k3][ps_, hh // 2, c * 64:(c + 1) * 64],
                       False, c == 1 and hh >= 2, [BSbf[sidx], Bqd[k3]], [PB[obk[hh % 2]]])
                db = dsb[c]
                for hp in range(2):
                    dprev = DECF[:, (gc - 1) * 2 + hp:(gc - 1) * 2 + hp + 1] if not first else DECF[:, gc * 2 + hp: gc * 2 + hp + 1]
                    dcur = DECF[:, gc * 2 + hp: gc * 2 + hp + 1]
                    OP("dve", "scalar_tensor_tensor", [BPst, Bdecf, PB[db]], [BPst], out=Pst[:, hp * 256:(hp + 1) * 256],
                       in0=Pst[:, hp * 256:(hp + 1) * 256], scalar=dprev, op0=ALU.mult, in1=pb[db][:, hp * 256:(hp + 1) * 256], op1=ALU.add)
                    OP("dve", "tensor_scalar", [BPst, Bdecf], [BSbf[1 - sidx]], out=Sbf[1 - sidx][:, hp * 256:(hp + 1) * 256],
                       in0=Pst[:, hp * 256:(hp + 1) * 256], scalar1=dcur, scalar2=None, op0=ALU.mult)
            for par in range(2):
                ACT(oloc[k2].rearrange("p (g e t) -> p g e t", e=2, t=128)[:, :, par, :], ovb[par], AF.Copy,
                    [PB[obk[par]]], [Boloc[k2]])
            DMA("sp", O_s[gsc], oloc[k2], [Boloc[k2]], [Bscr["O"]], Boloc[k2])

        try:
            p1_prep(0)
            def adv(g):
                if g is not None:
                    try:
                        next(g)
                    except StopIteration:
                        pass

            p1slot = {}

            def p1_prep_part(t, part):
                if t >= NSUB:
                    return
                idx, sc, _sf = subs[t]
                if idx + 1 >= len(tiles):
                    return
                s_, ti_, t0_ = tiles[idx + 1]
                if part == 0:
                    p1slot[(idx + 1, sc)] = cnt[0] % 2
                    cnt[0] += 1
                prep_sub(XIN, BXIN, t0_, s_, (idx + 1) % 2, modA1[l], modB1[l], sc, p1slot[(idx + 1, sc)], part)

            for t in range(NSUB + 2):
                gB = p1_B(t - 1) if 0 <= t - 1 < NSUB else None
                gA = p1_A(t) if t < NSUB else None
                p1_prep_part(t, 0)
                adv(gB)
                if t < NSUB:
                    idx, sc, _sf = subs[t]
                    if sc == 0:
                        p1_phaseA(idx)
                adv(gA)
                p1_prep_part(t, 1)
                adv(gB)
                if 0 <= t - 2 < NSUB:
                    p1_C(t - 2)
                adv(gB)
                adv(gA)
                p1_prep_part(t, 2)
                adv(gA)
        except _Stop:
            P.barrier(toks)
            break
        apos[0] = p1
        P.barrier(toks)
        if STOP == "p1":
            break

        dv = w_down[l].rearrange("(k p) c -> p k c", p=128)
        for k0 in range(0, 22, 2):
            DMA("pool", wd[:, k0:k0 + 2, :], dv[:, k0:k0 + 2, :], [], [BW0], BW0)
        NS2 = 4
        x2 = [a32(D) for _ in range(NS2)]; Bx2 = [Buf("x2_%d" % i) for i in range(NS2)]
        aL = [a16(512) for _ in range(NS2)]; BaL = [Buf("aL%d" % i) for i in range(NS2)]
        oL = [a32(512) for _ in range(NS2)]; BoL = [Buf("oL%d" % i) for i in range(NS2)]
        srL = [a16(512) for _ in range(NS2)]; BsrL = [Buf("srL%d" % i) for i in range(NS2)]
        vL = [a16(512) for _ in range(NS2)]; BvL = [Buf("vL%d" % i) for i in range(NS2)]
        kdL = [a16(256).rearrange("p (a t) -> p a t", a=2) for _ in range(NS2)]; BkdL = [Buf("kdL%d" % i) for i in range(NS2)]
        qdL = [a16(256).rearrange("p (a t) -> p a t", a=2) for _ in range(NS2)]; BqdL = [Buf("qdL%d" % i) for i in range(NS2)]
        Pb = a32(512); BPb = Buf("Pb")
        Sb = [a16(512) for _ in range(2)]; BSb = [Buf("Sb0"), Buf("Sb1")]
        osum = [a32(512) for _ in range(2)]; Bosum = [Buf("osum0"), Buf("osum1")]
        osq = a16(512); Bosq = Buf("osq")
        ro = a32(512); Bro = Buf("ro")
        on = a32(512); Bon = Buf("on")
        oTn = [a16(512) for _ in range(2)]; BoTn = [Buf("oTn0"), Buf("oTn1")]

        order = []
        for s in range(S):
            r0 = soff[s]
            nsc = SEQS[s] // 128
            for j in range(nsc - 1, -1, -1):
                order.append((s, r0 // 128 + j, j == nsc - 1))
        N2 = len(order)

        def p2_load(n):
            s, gsc, lastsc = order[n]
            k = n % NS2
            DMA("sp", x2[k], XIN[gsc * 128:(gsc + 1) * 128, :], [BXIN], [Bx2[k]], Bx2[k])
            DMA("sp", aL[k], A_s[gsc], [Bscr["A"]], [BaL[k]], BaL[k])
            DMA("sp", oL[k], O_s[gsc], [Bscr["O"]], [BoL[k]], BoL[k])
            DMA("sp", srL[k], SR_s[gsc], [Bscr["SR"]], [BsrL[k]], BsrL[k])
            DMA("sp", vL[k], V_s[gsc], [Bscr["V"]], [BvL[k]], BvL[k])
            DMA("sp", kdL[k], KD_s[gsc].rearrange("p (a t) -> p a t", a=2), [Bscr["KD"]], [BkdL[k]], BkdL[k])
            DMA("sp", qdL[k], QD_s[gsc].rearrange("p (a t) -> p a t", a=2), [Bscr["QD"]], [BqdL[k]], BqdL[k])

        ovb2 = (pb[0][:, 0:256].rearrange("p (h t) -> p h t", h=2), pb[3][:, 0:256].rearrange("p (h t) -> p h t", h=2))
        obk2 = (0, 3)

        def p2_A(n, part):
            s, gsc, lastsc = order[n]
            k = n % NS2
            ko = n % 2
            if part == 0:
                if lastsc:
                    OP("dve", "memset", [], [BPb], Pb, 0.0)
                    OP("dve", "memset", [], [BSb[0]], Sb[0], 0.0)
                    OP("dve", "memset", [], [BSb[1]], Sb[1], 0.0)
                for c in (1, 0):
                    db = 1 + c
                    for hp in range(2):
                        MM(pb[db][:, hp * 256:(hp + 1) * 256], kdL[k][c * 64:(c + 1) * 64, hp, :],
                           vL[k][c * 64:(c + 1) * 64, hp * 256:(hp + 1) * 256], True, True, [BkdL[k], BvL[k]], [PB[db]])
            c = 1 if part == 0 else 0
            gc = gsc * 2 + c
            first = lastsc and c == 1
            sidx = gc % 2
            db = 1 + c
            for hh in range(4):
                ps_ = slice((hh % 2) * 64, (hh % 2) * 64 + 64)
                col = (hh // 2) * 256 + (hh % 2) * 128
                MM(ovb2[hh % 2][:, hh // 2, c * 64:(c + 1) * 64], Sb[sidx][ps_, col:col + 128],
                   qdL[k][ps_, hh // 2, c * 64:(c + 1) * 64],
                   True, True, [BSb[sidx], BqdL[k]], [PB[obk2[hh % 2]]])
            for hp in range(2):
                dprev = DECB[:, (gc + 1) * 2 + hp:(gc + 1) * 2 + hp + 1] if not first else DECB[:, gc * 2 + hp: gc * 2 + hp + 1]
                dcur = DECB[:, gc * 2 + hp: gc * 2 + hp + 1]
                OP("dve", "scalar_tensor_tensor", [BPb, Bdecb, PB[db]], [BPb], out=Pb[:, hp * 256:(hp + 1) * 256],
                   in0=Pb[:, hp * 256:(hp + 1) * 256], scalar=dprev, op0=ALU.mult, in1=pb[db][:, hp * 256:(hp + 1) * 256], op1=ALU.add)
                OP("dve", "tensor_scalar", [BPb, Bdecb], [BSb[1 - sidx]], out=Sb[1 - sidx][:, hp * 256:(hp + 1) * 256],
                   in0=Pb[:, hp * 256:(hp + 1) * 256], scalar1=dcur, scalar2=None, op0=ALU.mult)
            if part == 1:
                for par in range(2):
                    OP("dve", "tensor_tensor", [PB[obk2[par]], BoL[k]], [Bosum[ko]],
                       out=osum[ko].rearrange("p (g e t) -> p g e t", e=2, t=128)[:, :, par, :], in0=ovb2[par],
                       in1=oL[k].rearrange("p (g e t) -> p g e t", e=2, t=128)[:, :, par, :], op=ALU.add)

        def p2_B(n):
            s, gsc, lastsc = order[n]
            k = n % NS2
            ko = n % 2
            ACT(osq, osum[ko], AF.Square, [Bosum[ko]], [Bosq])
            MM(pb[6], ones128, osq, True, True, [Bosq, Bcb], [PB[6]])
            ACT(ro, pb[6], AF.Ln, [PB[6]], [Bro], bias=EPS)
            ACT(ro, ro, AF.Exp, [Bro], [Bro], scale=-0.5)
            yield
            for hh in range(4):
                OP("dve", "scalar_tensor_tensor", [Bosum[ko], Bpar, Bro], [Bon], out=on[:, hh * 128:(hh + 1) * 128],
                   in0=osum[ko][:, hh * 128:(hh + 1) * 128], scalar=gh[:, l * 12 + 8 + hh: l * 12 + 9 + hh], op0=ALU.mult,
                   in1=ro[:, hh * 128:(hh + 1) * 128], op1=ALU.mult)
            OP("pool", "tensor_tensor", [Bon, BsrL[k]], [BoTn[ko]], out=oTn[ko], in0=on, in1=srL[k], op=ALU.mult)

        def p2_C(n, part):
            s, gsc, lastsc = order[n]
            k = n % NS2
            ko = n % 2
            yb = (4, 5)
            if part == 0 and lastsc:
                load_G(l, 0, s)
            half = part
            for kc in range(8):
                lhs = aL[k][:, kc * 128:(kc + 1) * 128] if kc < 4 else oTn[ko][:, (kc - 4) * 128:(kc - 3) * 128]
                MM(pb[yb[half]], lhs, wo[:, kc, half * 512:(half + 1) * 512], kc == 0, kc == 7,
                   [BaL[k], BoTn[ko], BW1], [PB[yb[half]]])
            if part == 1:
                post_res_a(yb, ko)

        def p2_Cpost(n):
            s, gsc, lastsc = order[n]
            post_res_b((4, 5), x2[n % NS2], Bx2[n % NS2], n % 2, XA, BXA, gsc * 128)

        p2_load(0)
        if N2 > 1:
            p2_load(1)
        def adv2(g):
            if g is not None:
                try:
                    next(g)
                except StopIteration:
                    pass

        for t in range(N2 + 2):
            gB2 = p2_B(t - 1) if 0 <= t - 1 < N2 else None
            adv2(gB2)
            if t < N2:
                p2_A(t, 0)
            if 0 <= t - 2 < N2:
                p2_C(t - 2, 0)
            if t < N2:
                p2_A(t, 1)
            if 0 <= t - 2 < N2:
                p2_C(t - 2, 1)
            adv2(gB2)
            if 0 <= t - 2 < N2:
                p2_Cpost(t - 2)
            if t + 2 < N2:
                p2_load(t + 2)
        apos[0] = p1
        P.barrier(toks)
        if STOP == "p2":
            break

        act = a16(22 * NT).rearrange("p (k t) -> p k t", k=22); Bact = [Buf("act%d" % i) for i in range(4)]
        wup = [a16(4096).rearrange("p (k a c) -> p k a c", k=8, a=2) for _ in range(3)]; Bwup = [Buf("wup%d" % i) for i in range(3)]
        cg = [a32(NT) for _ in range(2)]; cv = [a32(NT) for _ in range(2)]; sg = [a32(NT) for _ in range(2)]
        Bcg = [Buf("cg%d" % i) for i in range(2)]; Bcv = [Buf("cv%d" % i) for i in range(2)]; Bsg = [Buf("sg%d" % i) for i in range(2)]
        xh = a32(D); Bxh = Buf("xh")
        hsb = [a32(2) for _ in range(4)]; Bhsb = [Buf("hsb%d" % i) for i in range(4)]
        cnt3 = [0]
        gcount = [0]
        upn = [0]
        pn = [0]

        hnh = a16(D); Bhnh = Buf("hnh")
        ssqh = a32(1); rsdh = a32(1); Bssh_ = Buf("ssqh"); Brsh = Buf("rsdh")
        OP("pool", "memset", [], [Bxh], xh[0:2, :], 0.0)
        p3slot = {}

        def p3_prep_step(idx, g):
            s, ti, t0 = tiles[idx]
            hsl = idx % 2
            hh_ = hTh[hsl]
            lo_ok = ti > 0
            hi_ok = (ti + 1) * NT < SEQS[s]
            if g == 0:
                if lo_ok:
                    DMA("sp", xh[0:1, :], XA[t0 - 1:t0, :], [BXA], [Bxh], Bxh)
                if hi_ok:
                    DMA("sp", xh[1:2, :], XA[t0 + NT:t0 + NT + 1, :], [BXA], [Bxh], Bxh)
                ACT(junk[0:2, :], xh[0:2, :], AF.Square, [Bxh], [Bjunk, Bssh_], accum_out=ssqh[0:2, :])
                RSQ(rsdh[0:2, :], ssqh[0:2, :], 1.0 / D, [Bssh_], [Brsh], np_=2)
            if g == 1:
                ACT(hnh[0:2, :], xh[0:2, :], AF.Copy, [Bxh, Brsh], [Bhnh], scale=rsdh[0:2, :])
            if g == 2:
                psH = pbh[0][:, 0:16]
                for kc in range(8):
                    TR(psH[:, 2 * kc:2 * kc + 2], hnh[0:2, kc * 128:(kc + 1) * 128], identb[0:2, 0:2], [Bhnh, Bcb], [PB[0]])
                for kc in range(8):
                    OP("dve", "tensor_scalar", [PB[0], Bmod], [BhTh[hsl]], out=hh_[:, kc, :],
                       in0=psH[:, 2 * kc:2 * kc + 2], scalar1=modA2[l][:, kc * S + s: kc * S + s + 1],
                       scalar2=modB2[l][:, kc * S + s: kc * S + s + 1], op0=ALU.mult, op1=ALU.add)
                if not lo_ok:
                    OP("dve", "memset", [], [BhTh[hsl]], hh_[:, :, 0:1], 0.0)
                if not hi_ok:
                    OP("dve", "memset", [], [BhTh[hsl]], hh_[:, :, 1:2], 0.0)
            if g in (1, 3, 5, 7):
                sc = (g - 1) // 2
                sl = cnt3[0] % 2
                cnt3[0] += 1
                p3slot[(idx, sc)] = sl
                prep_sub(XA, BXA, t0, s, hsl, modA2[l], modB2[l], sc, sl, 0)
            if g in (2, 4, 6, 8):
                sc = (g - 2) // 2
                prep_sub(XA, BXA, t0, s, hsl, modA2[l], modB2[l], sc, p3slot[(idx, sc)], 1)
            if g in (3, 5, 7, 9):
                sc = (g - 3) // 2
                prep_sub(XA, BXA, t0, s, hsl, modA2[l], modB2[l], sc, p3slot[(idx, sc)], 2)

        def p3_prep(idx):
            for g in range(10):
                p3_prep_step(idx, g)

        def p3_loadw(gi):
            g = gi % 11
            sl = gi % 3
            DMA("sp", wup[sl].rearrange("p k a c -> p (k a c)"), WUP_s[l, g], [BWUP], [Bwup[sl]], Bwup[sl])

        ngroups = 11 * len(tiles)

        def p3_main(idx):
            s, ti, t0 = tiles[idx]
            hsl = idx % 2
            h = hT[hsl]
            hh_ = hTh[hsl]
            hvs = [pb[4 + i][:, 0:176].rearrange("p (m c) -> p m c", m=44) for i in range(2)]
            if ti == 0:
                load_G(l, 1, s)
            for g in range(11):
                gi = idx * 11 + g
                if gi + 2 < ngroups:
                    p3_loadw(gi + 2)
                sl = gi % 3
                for pi in range(2):
                    m = g * 2 + pi
                    k2 = pn[0] % 2
                    pn[0] += 1
                    res = []
                    for a in range(2):
                        b = 1 + upn[0] % 3
                        hb = 4 + upn[0] % 2
                        hv = hvs[upn[0] % 2]
                        upn[0] += 1
                        mm_ = m + 22 * a
                        for kc in range(8):
                            MM(pb[b], wup[sl][:, kc, a, pi * 128:(pi + 1) * 128], h[:, kc, :], kc == 0, kc == 7,
                               BhT[hsl] + [Bwup[sl]], [PB[b]])
                        for kc in range(8):
                            MM(hv[:, mm_, 0:2], wup[sl][:, kc, a, pi * 128:(pi + 1) * 128], hh_[:, kc, :], kc == 0, kc == 7,
                               [BhTh[hsl], Bwup[sl]], [PB[hb]])
                        dst, Bd = (cg[k2], Bcg[k2]) if a == 0 else (cv[k2], Bcv[k2])
                        cp = cvp[:, l * 176 + mm_ * 4: l * 176 + mm_ * 4 + 4]
                        ACT(dst, pb[b], AF.Identity, [PB[b], Bpar], [Bd], scale=cp[:, 1:2], bias=cp[:, 3:4])
                        OP("dve", "scalar_tensor_tensor", [PB[b], Bpar, Bd], [Bd], out=dst[:, 1:NT], in0=pb[b][:, 0:NT - 1],
                           scalar=cp[:, 0:1], op0=ALU.mult, in1=dst[:, 1:NT], op1=ALU.add)
                        OP("dve", "scalar_tensor_tensor", [PB[b], Bpar, Bd], [Bd], out=dst[:, 0:NT - 1], in0=pb[b][:, 1:NT],
                           scalar=cp[:, 2:3], op0=ALU.mult, in1=dst[:, 0:NT - 1], op1=ALU.add)
                        hsl_ = (upn[0] - 1) % 4
                        ACT(hsb[hsl_], hv[:, mm_, 0:2], AF.Copy, [PB[hb]], [Bhsb[hsl_]])
                        OP("dve", "scalar_tensor_tensor", [Bhsb[hsl_], Bpar, Bd], [Bd], out=dst[:, 0:1], in0=hsb[hsl_][:, 0:1],
                           scalar=cp[:, 0:1], op0=ALU.mult, in1=dst[:, 0:1], op1=ALU.add)
                        OP("dve", "scalar_tensor_tensor", [Bhsb[hsl_], Bpar, Bd], [Bd], out=dst[:, NT - 1:NT], in0=hsb[hsl_][:, 1:2],
                           scalar=cp[:, 2:3], op0=ALU.mult, in1=dst[:, NT - 1:NT], op1=ALU.add)
                    ACT(sg[k2], cg[k2], AF.Silu, [Bcg[k2]], [Bsg[k2]])
                    OP("pool", "tensor_tensor", [Bsg[k2], Bcv[k2]], Bact, out=act[:, m, :], in0=sg[k2], in1=cv[k2], op=ALU.mult)
                if idx + 1 < len(tiles):
                    p3_prep_step(idx + 1, g)
            pend3 = []
            for sc in range(4):
                k = cnt3x[0] % 2
                cnt3x[0] += 1
                r0 = t0 + sc * 128
                DMA("sp", xr[k], XA[r0:r0 + 128, :], [BXA], [Bxr[k]], Bxr[k])
                yb = (6, 7) if sc % 2 == 0 else (1, 2)
                for half in range(2):
                    for m in range(22):
                        MM(pb[yb[half]], act[:, m, sc * 128:(sc + 1) * 128], wd[:, m, half * 512:(half + 1) * 512], m == 0, m == 21,
                           Bact + [BW0], [PB[yb[half]]])
                post_res_a(yb, k)
                if pend3:
                    post_res_b(*pend3.pop())
                pend3.append((yb, xr[k], Bxr[k], k, XOUT3, BXOUT3, r0))
            post_res_b(*pend3.pop())

        xr = [a32(D) for _ in range(2)]; Bxr = [Buf("xr%d" % i) for i in range(2)]
        cnt3x = [0]

        def post_res3(yb, k, r0):
            for half in range(2):
                ACT(junk[:, half * 512:(half + 1) * 512], pb[yb[half]], AF.Square, [PB[yb[half]]], [Bjunk, Bssh[k]],
                    accum_out=ssh[k][:, half:half + 1])
            OP("dve", "tensor_tensor", [Bssh[k]], [Bss2[k]], out=ss2[k], in0=ssh[k][:, 0:1], in1=ssh[k][:, 1:2], op=ALU.add)
            ACT(ry[k], ss2[k], AF.Sqrt, [Bss2[k]], [Bry[k]], scale=1.0 / D, bias=EPS)
            OP("dve", "reciprocal", [Bry[k]], [Bry[k]], out=ry[k], in_=ry[k])
            for half in range(2):
                OP("dve", "scalar_tensor_tensor", [PB[yb[half]], Bry[k], BG], [Btt[k]],
                   out=tt_[k][:, half * 512:(half + 1) * 512], in0=pb[yb[half]], scalar=ry[k], op0=ALU.mult,
                   in1=Gt[:, half * 512:(half + 1) * 512], op1=ALU.mult)
            OP("pool", "tensor_tensor", [Btt[k], Bxr[k]], [Btt[k]], out=tt_[k], in0=tt_[k], in1=xr[k], op=ALU.add)
            DMA("sp", XOUT3[r0:r0 + 128, :], tt_[k], [Btt[k]], [BXOUT3], Btt[k])

        p3_loadw(0)
        p3_loadw(1)
        p3_prep(0)
        for idx in range(len(tiles)):
            p3_main(idx)
        apos[0] = p1
        P.barrier(toks)

    P.emit(final_bufs=[BY])
    st.close()
    return nc


def make_consts():
    j = np.arange(128)[:, None]
    i = np.arange(128)[None, :]
    same = (j // 64) == (i // 64)
    triU = (same & (j <= i)).astype(np.float32)
    triL = (same & (j >= i)).astype(np.float32)
    c = np.zeros((128, 640), np.float32)
    c[:, 0:128] = np.eye(128, dtype=np.float32)
    c[:, 128:256] = triU * (-1.0 / 16)
    c[:, 256:384] = triL * (-1.0 / 16)
    c[:, 384:512] = triU
    c[:, 512:640] = triL
    return c


def pack_shared(w_ada, b_ada, g_pre_mix, g_post_mix, g_pre_ffn, g_post_ffn, w_in, w_s, b_s, g_vn, g_out_a,
                w_gf, b_gf, w_gb, b_gb, g_out_b, w_out, w_up, w_conv, b_conv, w_down):
    L = w_ada.shape[0]
    f = lambda a: np.ascontiguousarray(a, dtype=np.float32)
    fm8 = lambda v: v.reshape(L, 8, 128).transpose(0, 2, 1)
    fm4 = lambda v: v.reshape(L, 4, 128).transpose(0, 2, 1)
    d = {}
    d["w_ada"] = f(w_ada)
    d["badaF"] = f(b_ada.reshape(L, 48, 128).transpose(0, 2, 1))
    d["badaR"] = f(np.stack([b_ada[:, 2048:3072], b_ada[:, 5120:6144]], axis=1))
    d["gpreF"] = f(np.concatenate([fm8(g_pre_mix), fm8(g_pre_ffn)], axis=2))
    d["gpostR"] = f(np.stack([g_post_mix, g_post_ffn], axis=1))
    d["w_in"] = f(w_in)
    d["wsT"] = f(w_s.transpose(0, 3, 1, 2).reshape(L, 128, 512))
    d["bsR"] = f(b_s.reshape(L, 1, 512))
    d["ghF"] = f(np.concatenate([fm4(g_vn), fm4(g_out_a), fm4(g_out_b)], axis=2))
    d["w_g"] = f(np.stack([w_gf, w_gb], axis=1))
    d["bgR"] = f(np.concatenate([b_gf, b_gb], axis=1).reshape(L, 1, 512))
    d["w_out"] = f(w_out)
    d["w_up"] = f(w_up)
    d["w_down"] = f(w_down)
    cw = np.concatenate([w_conv, b_conv[:, None, :]], axis=1)
    d["convF"] = f(cw.reshape(L, 4, 44, 128).transpose(0, 3, 2, 1).reshape(L, 128, 176))
    d["consts"] = make_consts()
    return d


_NC_CACHE = {}


def kernel(x_prompt, x_sample, c_prompt, c_sample, w_ada, b_ada, g_pre_mix, g_post_mix,
           g_pre_ffn, g_post_ffn, w_in, w_s, b_s, g_vn, g_out_a, w_gf, b_gf, w_gb, b_gb,
           g_out_b, w_out, w_up, w_conv, b_conv, w_down):
    n = 8
    x_prompt = np.asarray(x_prompt, dtype=np.float32)
    x_sample = np.asarray(x_sample, dtype=np.float32)
    c_prompt = np.asarray(c_prompt, dtype=np.float32)
    c_sample = np.asarray(c_sample, dtype=np.float32)
    Bp, Tp, _ = x_prompt.shape
    Bs, Ts, _ = x_sample.shape
    ppc = Bp // n
    spc = Bs // n
    SEQS = [Tp] * ppc + [Ts] * spc
    L = int(np.asarray(w_ada).shape[0])
    shared = pack_shared(*[np.asarray(a, dtype=np.float32) for a in (
        w_ada, b_ada, g_pre_mix, g_post_mix, g_pre_ffn, g_post_ffn, w_in, w_s, b_s, g_vn, g_out_a,
        w_gf, b_gf, w_gb, b_gb, g_out_b, w_out, w_up, w_conv, b_conv, w_down)])
    key = (tuple(SEQS), L)
    if key not in _NC_CACHE:
        _NC_CACHE[key] = build_nc(SEQS, L)
    nc = _NC_CACHE[key]
    in_maps = []
    S = len(SEQS)
    for c in range(n):
        xs_ = [x_prompt[c * ppc + i] for i in range(ppc)] + [x_sample[c * spc + i] for i in range(spc)]
        cs_ = [c_prompt[c * ppc + i] for i in range(ppc)] + [c_sample[c * spc + i] for i in range(spc)]
        cc = np.stack(cs_, 0)
        cT = cc.reshape(S, 8, 128).transpose(2, 1, 0).reshape(128, 8 * S)
        m = dict(shared)
        m["x"] = np.ascontiguousarray(np.concatenate(xs_, 0))
        m["cT"] = np.ascontiguousarray(cT)
        in_maps.append(m)
    res = run_bass_kernel_spmd(nc, in_maps, core_ids=list(range(n)))
    yp = np.empty((Bp, Tp, D), np.float32)
    ys = np.empty((Bs, Ts, D), np.float32)
    for c in range(n):
        y = res.results[c]["y"]
        o = 0
        for i in range(ppc):
            yp[c * ppc + i] = y[o:o + Tp]
            o += Tp
        for i in range(spc):
            ys[c * spc + i] = y[o:o + Ts]
            o += Ts
    return (yp, ys)
```
